# Optimizing a Trainium2 kernel written in Bass

```python
import jax, jax.numpy as jnp
from jax import lax
import numpy as np

D_MODEL = 1024
BATCH = 16
SEQ = 2048
DEPTH = 1

HEAD_DIM = 64
N_Q_HEADS = 16
N_KV_HEADS = 2
GQA_GROUP = N_Q_HEADS // N_KV_HEADS
WINDOW = 128
BLOCK = 128
Q_WIDTH = N_Q_HEADS * HEAD_DIM
KV_WIDTH = N_KV_HEADS * HEAD_DIM
CONV_CH = D_MODEL
CONV_WIDTH = 31
N_BRANCH = 2
IN_WIDTH = Q_WIDTH + 2 * KV_WIDTH + 2 * CONV_CH + N_BRANCH * D_MODEL
D_FF = 2816
FFN_RESIDUAL = 0.5
N_MOD = 9
EPS = 1e-6

kernel_name = "conditioned_hybrid_swa_conformer_macaron_layer"


def rmsnorm(x, g):
    xf = x.astype(jnp.float32)
    y = xf * lax.rsqrt(jnp.mean(xf * xf, axis=-1, keepdims=True) + EPS)
    return (y * g.astype(jnp.float32)).astype(x.dtype)


def layernorm(x, g, b):
    xf = x.astype(jnp.float32)
    mu = jnp.mean(xf, axis=-1, keepdims=True)
    var = jnp.mean(jnp.square(xf - mu), axis=-1, keepdims=True)
    y = (xf - mu) * lax.rsqrt(var + EPS)
    return (y * g.astype(jnp.float32) + b.astype(jnp.float32)).astype(x.dtype)


def modulate(h, shift, scale):
    return h * (1 + scale[:, None, :]) + shift[:, None, :]


def swiglu(h, w_gate, w_up, w_down):
    return (jax.nn.silu(h @ w_gate) * (h @ w_up)) @ w_down


def sliding_window_sink_attention(q, k, v, sinks):
    B, S = q.shape[0], q.shape[1]
    nb = S // BLOCK
    qb = q.reshape(B, nb, BLOCK, N_KV_HEADS, GQA_GROUP, HEAD_DIM)

    def band(t):
        tp = jnp.pad(t, ((0, 0), (BLOCK, 0), (0, 0), (0, 0)))
        tb = tp.reshape(B, nb + 1, BLOCK, N_KV_HEADS, HEAD_DIM)
        return jnp.concatenate([tb[:, :-1], tb[:, 1:]], axis=2)

    kb, vb = band(k), band(v)
    scores = jnp.einsum('bnqkgd,bnskd->bnkgqs', qb, kb).astype(jnp.float32) * (HEAD_DIM ** -0.5)
    qi = jnp.arange(BLOCK)[:, None]
    sj = jnp.arange(2 * BLOCK)[None, :]
    rel = qi + BLOCK - sj
    key_pos = jnp.arange(nb)[:, None, None] * BLOCK + sj[None] - BLOCK
    valid = ((rel >= 0) & (rel < WINDOW))[None] & (key_pos >= 0)
    valid = valid[None, :, None, None]
    sink = sinks.astype(jnp.float32).reshape(1, 1, N_KV_HEADS, GQA_GROUP, 1, 1)
    masked = jnp.where(valid, scores, -jnp.inf)
    m = jnp.maximum(jnp.max(masked, axis=-1, keepdims=True), sink)
    p = jnp.where(valid, jnp.exp(masked - m), 0.0)
    denom = jnp.sum(p, axis=-1, keepdims=True) + jnp.exp(sink - m)
    probs = (p / denom).astype(v.dtype)
    out = jnp.einsum('bnkgqs,bnskd->bnqkgd', probs, vb)
    return out.reshape(B, S, Q_WIDTH)


def conformer_conv(u2, w_dw, b_dw, ln_g, ln_b, w_pw):
    a, b = jnp.split(u2, 2, axis=-1)
    u = a * jax.nn.sigmoid(b)
    u = jnp.pad(u, ((0, 0), (CONV_WIDTH - 1, 0), (0, 0)))
    y = lax.conv_general_dilated(
        u, w_dw[:, None, :].astype(u.dtype), window_strides=(1,), padding='VALID',
        dimension_numbers=('NWC', 'WIO', 'NWC'), feature_group_count=CONV_CH)
    y = y + b_dw
    y = jax.nn.silu(layernorm(y, ln_g, ln_b))
    return y @ w_pw


def setup_inputs(seed: int = 0) -> dict:
    key = jax.random.key(seed)
    ks = jax.random.split(key, 24)
    f32 = jnp.float32
    L = DEPTH

    def w(k, shape, fan_in):
        return jax.random.normal(k, shape, f32) * (fan_in ** -0.5)

    def gain(k, shape):
        return 1.0 + 0.02 * jax.random.normal(k, shape, f32)

    def small(k, shape):
        return 0.01 * jax.random.normal(k, shape, f32)

    return {
        "x": jax.random.normal(ks[0], (BATCH, SEQ, D_MODEL), f32),
        "c": jax.random.normal(ks[1], (BATCH, D_MODEL), f32),
        "w_ada": w(ks[2], (L, D_MODEL, N_MOD * D_MODEL), D_MODEL),
        "b_ada": small(ks[3], (L, N_MOD * D_MODEL)),
        "norm_ffn1_g": gain(ks[4], (L, D_MODEL)),
        "ffn1_w_gate": w(ks[5], (L, D_MODEL, D_FF), D_MODEL),
        "ffn1_w_up": w(ks[6], (L, D_MODEL, D_FF), D_MODEL),
        "ffn1_w_down": w(ks[7], (L, D_FF, D_MODEL), D_FF),
        "norm_mix_g": gain(ks[8], (L, D_MODEL)),
        "w_in": w(ks[9], (L, D_MODEL, IN_WIDTH), D_MODEL),
        "attn_sinks": 0.5 * jax.random.normal(ks[10], (L, N_Q_HEADS), f32),
        "w_attn_o": w(ks[11], (L, Q_WIDTH, D_MODEL), Q_WIDTH),
        "conv_w_dw": w(ks[12], (L, CONV_WIDTH, CONV_CH), CONV_WIDTH),
        "conv_b_dw": small(ks[13], (L, CONV_CH)),
        "conv_ln_g": gain(ks[14], (L, CONV_CH)),
        "conv_ln_b": small(ks[15], (L, CONV_CH)),
        "w_conv_o": w(ks[16], (L, CONV_CH, D_MODEL), CONV_CH),
        "w_out": w(ks[17], (L, D_MODEL, D_MODEL), D_MODEL),
        "norm_ffn2_g": gain(ks[18], (L, D_MODEL)),
        "ffn2_w_gate": w(ks[19], (L, D_MODEL, D_FF), D_MODEL),
        "ffn2_w_up": w(ks[20], (L, D_MODEL, D_FF), D_MODEL),
        "ffn2_w_down": w(ks[21], (L, D_FF, D_MODEL), D_FF),
        "final_norm_g": gain(ks[22], (D_MODEL,)),
    }


def reference(x, c, w_ada, b_ada, norm_ffn1_g, ffn1_w_gate, ffn1_w_up, ffn1_w_down,
              norm_mix_g, w_in, attn_sinks, w_attn_o, conv_w_dw, conv_b_dw, conv_ln_g,
              conv_ln_b, w_conv_o, w_out, norm_ffn2_g, ffn2_w_gate, ffn2_w_up, ffn2_w_down,
              final_norm_g):
    B, S, _ = x.shape
    c_act = jax.nn.silu(c)
    split_idx = np.cumsum([Q_WIDTH, KV_WIDTH, KV_WIDTH, 2 * CONV_CH, D_MODEL]).tolist()
    for l in range(DEPTH):
        mod = (c_act @ w_ada[l] + b_ada[l]).reshape(B, N_MOD, D_MODEL)
        sh1, sc1, g1 = mod[:, 0], mod[:, 1], mod[:, 2]
        sh2, sc2, g2 = mod[:, 3], mod[:, 4], mod[:, 5]
        sh3, sc3, g3 = mod[:, 6], mod[:, 7], mod[:, 8]

        h = modulate(rmsnorm(x, norm_ffn1_g[l]), sh1, sc1)
        x = x + FFN_RESIDUAL * g1[:, None, :] * swiglu(h, ffn1_w_gate[l], ffn1_w_up[l], ffn1_w_down[l])

        h = modulate(rmsnorm(x, norm_mix_g[l]), sh2, sc2)
        proj = h @ w_in[l]
        q, k, v, conv_in, gate_a, gate_c = jnp.split(proj, split_idx, axis=-1)
        q = q.reshape(B, S, N_Q_HEADS, HEAD_DIM)
        k = k.reshape(B, S, N_KV_HEADS, HEAD_DIM)
        v = v.reshape(B, S, N_KV_HEADS, HEAD_DIM)
        y_attn = sliding_window_sink_attention(q, k, v, attn_sinks[l]) @ w_attn_o[l]
        y_conv = conformer_conv(conv_in, conv_w_dw[l], conv_b_dw[l], conv_ln_g[l],
                                conv_ln_b[l], w_conv_o[l])
        merged = jax.nn.sigmoid(gate_a) * y_attn + jax.nn.sigmoid(gate_c) * y_conv
        x = x + g2[:, None, :] * (merged @ w_out[l])

        h = modulate(rmsnorm(x, norm_ffn2_g[l]), sh3, sc3)
        x = x + FFN_RESIDUAL * g3[:, None, :] * swiglu(h, ffn2_w_gate[l], ffn2_w_up[l], ffn2_w_down[l])
    return rmsnorm(x, final_norm_g)
```

```python
import contextlib
import numpy as np
import concourse.bass as bass
import concourse.mybir as mybir
from concourse.bass_utils import run_bass_kernel_spmd

F32 = mybir.dt.float32
BF16 = mybir.dt.bfloat16
AF = mybir.ActivationFunctionType
ALU = mybir.AluOpType
AX = mybir.AxisListType

NCORES = 8
D = 1024
KC = 8
DFF = 2816
FC = 22
SEQ = 2048
BPC = 2
T = 1024
NTB = 2
NPASS_PER_SEQ = SEQ // T
EPS = 1e-6
NSLOT = 4
CONVW = 31
NEG = -30000.0

V_G1, V_G2, V_G3, V_GF, V_BDW, V_LNG, V_LNB = 0, 8, 16, 24, 32, 40, 48
V_BADA = 56
V_C = V_BADA + 72
V_WDW = V_C + 16
V_SINK = V_WDW + 8 * CONVW
NV = V_SINK + 16
C_ID, C_ON, C_M0, C_M1 = 0, 128, 256, 512
NCONST = 768


def small_plan():
    p = []
    for j in range(FC):
        p += [("g1", j), ("u1", j)]
    seg1 = len(p)
    for c in range(KC):
        p += [("ca", c), ("cb", c)]
    for m in range(KC):
        p += [("gc", m), ("co", m)]
    for c in range(KC):
        p += [("q", c)]
    p += [("k", 0), ("k", 1), ("v", 0)]
    for m in range(KC):
        p += [("ga", m), ("ao", m)]
    for m in range(KC):
        p += [("wo", m)]
    seg2 = len(p)
    for j in range(FC):
        p += [("g2", j), ("u2", j)]
    return p, seg1, seg2


def _tile(Wc):
    din = Wc.shape[0]
    return np.ascontiguousarray(Wc.reshape(din // 128, 128, 128).transpose(1, 0, 2)).reshape(128, din)


class Sched:
    ENG = ("pe", "act", "dve", "pool", "sp")

    def __init__(self, nc, stack):
        self.nc = nc
        self.stack = stack
        self.prog = {e: [] for e in self.ENG}
        self.sem = {}
        self.cnt = {}
        self.waited = {}
        self.lastw = {}
        self.readers = {}
        for e in ("pe", "act", "dve", "pool"):
            self.source(e)

    def source(self, name):
        if name not in self.sem:
            self.sem[name] = self.stack.enter_context(self.nc.semaphore("s_" + name))
            self.cnt[name] = 0
        return name

    def op(self, eng, fn, reads=(), writes=(), src=None, inc=1):
        source = src or eng
        deps = {}

        def add(d, raw):
            s, v = d
            if s == eng and source == eng and not raw:
                return
            if deps.get(s, 0) < v:
                deps[s] = v

        for k in reads:
            if k in self.lastw:
                add(self.lastw[k], True)
        for k in writes:
            if k in self.lastw:
                add(self.lastw[k], False)
            for s, v in self.readers.get(k, {}).items():
                add((s, v), False)
        waits = []
        for s, v in deps.items():
            if self.waited.get((eng, s), 0) < v:
                self.waited[(eng, s)] = v
                waits.append((s, v))
        self.cnt[source] += inc
        v = self.cnt[source]
        self.prog[eng].append((waits, fn, source, inc))
        for k in writes:
            self.lastw[k] = (source, v)
            self.readers[k] = {}
        for k in reads:
            self.readers.setdefault(k, {})[source] = v
        return v

    def final_wait(self, eng, sources):
        waits = [(s, self.cnt[s]) for s in sources if self.cnt[s] > 0]
        self.prog[eng].append((waits, None, None, 0))

    def emit(self, block):
        def mk(e):
            def body(eng):
                for waits, fn, source, inc in self.prog[e]:
                    for s, v in waits:
                        eng.wait_ge(self.sem[s], v)
                    if fn is not None:
                        ins = fn(eng)
                        ins.then_inc(self.sem[source], inc)
            return body
        block.tensor(mk("pe"))
        block.scalar(mk("act"))
        block.vector(mk("dve"))
        block.gpsimd(mk("pool"))
        block.sync(mk("sp"))


def build_program(stages=("ffn1", "mixer", "ffn2", "norm"), npass=4):
    nc = bass.Bass("TRN2", target_bir_lowering=False)
    plan, seg1, seg2 = small_plan()
    NSM = len(plan)
    x_d = nc.dram_tensor("x", [BPC, SEQ, D], F32, kind="ExternalInput").ap()
    wsm_d = nc.dram_tensor("wsm", [NSM, 128, D], F32, kind="ExternalInput").ap()
    wlg_d = nc.dram_tensor("wlg", [16, 128, DFF], F32, kind="ExternalInput").ap()
    wada_d = nc.dram_tensor("wada", [72, 128, D], F32, kind="ExternalInput").ap()
    vecs_d = nc.dram_tensor("vecs", [128, NV], F32, kind="ExternalInput").ap()
    consts_d = nc.dram_tensor("consts", [128, NCONST], F32, kind="ExternalInput").ap()
    out_d = nc.dram_tensor("out", [BPC, SEQ, D], F32, kind="ExternalOutput").ap()
    if "dbg" in stages:
        dbg_d = nc.dram_tensor("dbg", [4, 128, KC, T], BF16, kind="ExternalOutput").ap()

    with contextlib.ExitStack() as st:
        def sb(name, shape, dt):
            return st.enter_context(nc.sbuf_tensor(name, shape, dt))

        def psum(name, shape, dt):
            return st.enter_context(nc.psum_tensor(name, shape, dt))

        xT = sb("xT", [128, KC, T], F32)
        bufH = sb("bufH", [128, KC, T], BF16)
        bufA = sb("bufA", [128, KC, T], BF16)
        bufB = sb("bufB", [128, KC, T], BF16)
        bufC = sb("bufC", [128, KC, T], BF16)
        wring = sb("wring", [128, NSLOT, 4 * D], BF16)
        xstage = sb("xstage", [128, 2, D], F32)
        ostage = sb("ostage", [128, 2, D], F32)
        vecs = sb("vecs_sb", [128, NV], F32)
        consts = sb("consts_sb", [128, NCONST], F32)
        identb = sb("identb", [128, 128], BF16)
        onesb = sb("onesb", [128, 128], BF16)
        maskb = sb("maskb", [128, 2, 256], BF16)
        cact = sb("cact", [128, KC, BPC], BF16)
        modsb = sb("modsb", [128, 72, BPC], F32)
        AV = sb("AV", [128, 3, 3, KC, BPC], F32)
        kdup = sb("kdup", [128, 2, 128 + T], BF16)
        vtok = sb("vtok", [128, T // 128 + 1, 128], BF16)
        ubuf = sb("ubuf", [128, 2, 30 + T], BF16)
        uhalo = sb("uhalo", [128, KC, 30], BF16)
        diag = sb("diag", [128, 2, CONVW, 128], BF16)
        sq = sb("sq", [128, 3, 512], BF16)
        sd = sb("sd", [128, 2, 512], F32)
        rstd = sb("rstd", [128, NTB, 512], F32)
        tmpf = sb("tmpf", [128, 4, 512], F32)
        lmean = sb("lmean", [128, 512], F32)
        E_sb = sb("E_sb", [128, 2, 2, 256], F32)
        Pn = sb("Pn", [128, 2, 2, 256], BF16)
        PTs = sb("PTs", [128, 2, 4, 128], BF16)
        sstat = sb("sstat", [128, 2, 8, 2], F32)

        pbank = [psum("ps%d" % i, [128, 512], F32) for i in range(7)]
        ptb = psum("ps7", [128, 2, 4, 128], BF16)

        S = Sched(nc, st)
        for nm in ("dma_c", "dma_x0", "dma_x1", "dma_o0", "dma_o1"):
            S.source(nm)
        for s in range(NSLOT):
            S.source("dma_w%d" % s)

        groups = []
        for g0 in range(0, 72, 4):
            groups.append(("ada", g0, 4, [("ada", n) for n in range(g0, g0 + 4)]))
        for p in range(npass):
            def seg(a, b):
                i = a
                while i < b:
                    n = min(4, b - i)
                    groups.append(("sm", i, n, plan[i:i + n]))
                    i += n
            seg(0, seg1)
            for m in range(8):
                groups.append(("lg", m, 1, [("d1", m)]))
            seg(seg1, seg2)
            seg(seg2, NSM)
            for m in range(8):
                groups.append(("lg", 8 + m, 1, [("d2", m)]))
        wstate = {"issued": 0, "g": 0, "t": 0}

        def w_issue(upto):
            while wstate["issued"] < min(upto, len(groups)):
                gi = wstate["issued"]
                kind, start, n, _ = groups[gi]
                slot = gi % NSLOT
                if kind == "lg":
                    src_ap = wlg_d[start]
                    dst = wring[:, slot, 0:DFF]
                else:
                    dram = wada_d if kind == "ada" else wsm_d
                    src_ap = dram[start:start + n].rearrange("t p f -> p t f")
                    dst = wring[:, slot, 0:n * D].rearrange("p (t f) -> p t f", t=n)

                def fn(e, dst=dst, src_ap=src_ap):
                    return e.dma_start(out=dst, in_=src_ap)
                S.op("pool", fn, reads=(), writes=[("w", slot)], src="dma_w%d" % slot, inc=16)
                wstate["issued"] += 1

        def w_take(desc):
            gi, ti = wstate["g"], wstate["t"]
            kind, start, n, descs = groups[gi]
            assert descs[ti] == desc, (descs[ti], desc)
            w_issue(gi + NSLOT - 1)
            slot = gi % NSLOT
            if kind == "lg":
                view = wring[:, slot, 0:DFF]
            else:
                view = wring[:, slot, ti * D:(ti + 1) * D]
            wstate["t"] += 1
            if wstate["t"] == n:
                wstate["t"] = 0
                wstate["g"] += 1
            return view, ("w", slot)

        rr = {"sq": 0, "tmp": 0, "ev": 0, "mmA": 0, "mmB": 0}

        def nxt(name, n):
            v = rr[name]
            rr[name] = (v + 1) % n
            return v

        def mm_group(out_ap, pairs, reads, writes):
            def fn(pe):
                n = len(pairs)
                ins = None
                for i, (l, r) in enumerate(pairs):
                    ins = pe.matmul(out_ap, l, r, start=(i == 0), stop=(i == n - 1))
                return ins
            S.op("pe", fn, reads, writes)

        def act_op(out, in_, func, reads, writes, bias=None, scale=None, accum_out=None):
            kw = {}
            if bias is not None:
                kw["bias"] = bias
            if scale is not None:
                kw["scale"] = scale
            if accum_out is not None:
                kw["accum_out"] = accum_out
            S.op("act", lambda e: e.activation(out=out, in_=in_, func=func, **kw), reads, writes)

        def tt(eng, out, in0, in1, op, reads, writes):
            S.op(eng, lambda e: e.tensor_tensor(out=out, in0=in0, in1=in1, op=op), reads, writes)

        def stt(out, in0, scalar, in1, op0, op1, reads, writes):
            S.op("dve", lambda e: e.scalar_tensor_tensor(out=out, in0=in0, scalar=scalar, in1=in1,
                                                         op0=op0, op1=op1), reads, writes)

        def copy_alt(out, in_, reads, writes):
            if nxt("ev", 2) == 0:
                S.op("act", lambda e: e.copy(out=out, in_=in_), reads, writes)
            else:
                S.op("dve", lambda e: e.tensor_copy(out=out, in_=in_), reads, writes)

        def bankA():
            i = nxt("mmA", 2)
            return pbank[i], ("ps", i)

        def bankB():
            i = 2 + nxt("mmB", 2)
            return pbank[i], ("ps", i)

        def tsl(tb):
            return slice(tb * 512, (tb + 1) * 512)

        identf = consts[:, C_ID:C_ID + 128]

        S.op("sp", lambda e: e.dma_start(out=vecs[:], in_=vecs_d), writes=[("vecs",)], src="dma_c", inc=16)
        S.op("sp", lambda e: e.dma_start(out=consts[:], in_=consts_d), writes=[("consts",)], src="dma_c", inc=16)
        S.op("dve", lambda e: e.tensor_copy(out=identb[:], in_=consts[:, C_ID:C_ID + 128]),
             reads=[("consts",)], writes=[("identb",)])
        S.op("dve", lambda e: e.tensor_copy(out=onesb[:], in_=consts[:, C_ON:C_ON + 128]),
             reads=[("consts",)], writes=[("onesb",)])
        S.op("dve", lambda e: e.tensor_copy(out=maskb[:].rearrange("p a b -> p (a b)"),
                                            in_=consts[:, C_M0:C_M0 + 512]),
             reads=[("consts",)], writes=[("maskb",)])
        act_op(cact[:].rearrange("p c b -> p (c b)"), vecs[:, V_C:V_C + 16], AF.Silu,
               reads=[("vecs",)], writes=[("cact",)])
        mod_ps = pbank[4]
        for n in range(72):
            wv_, wk_ = w_take(("ada", n))
            pairs = [(wv_[:, k * 128:(k + 1) * 128], cact[:, k, :]) for k in range(KC)]
            mm_group(mod_ps[:, n * 2:(n + 1) * 2], pairs, reads=[wk_, ("cact",)], writes=[("ps", 4)])
        tt("dve", modsb[:], mod_ps[:, 0:144].rearrange("p (n b) -> p n b", b=BPC),
           vecs[:, V_BADA:V_BADA + 72].unsqueeze(2).to_broadcast([128, 72, BPC]), ALU.add,
           reads=[("ps", 4), ("vecs",)], writes=[("modsb",)])
        for sub, (vg, gscale) in enumerate(((V_G1, 0.5), (V_G2, 1.0), (V_G3, 0.5))):
            for b in range(BPC):
                stt(AV[:, sub, 0, :, b], modsb[:, (3 * sub + 1) * 8:(3 * sub + 2) * 8, b], 1.0,
                    vecs[:, vg:vg + 8], ALU.add, ALU.mult, reads=[("modsb",), ("vecs",)], writes=[("AV",)])
                S.op("dve", lambda e, sub=sub, b=b: e.tensor_copy(
                    out=AV[:, sub, 1, :, b], in_=modsb[:, (3 * sub) * 8:(3 * sub + 1) * 8, b]),
                    reads=[("modsb",)], writes=[("AV",)])
                S.op("dve", lambda e, sub=sub, b=b, gscale=gscale: e.tensor_scalar(
                    out=AV[:, sub, 2, :, b], in0=modsb[:, (3 * sub + 2) * 8:(3 * sub + 3) * 8, b],
                    scalar1=gscale, scalar2=None, op0=ALU.mult),
                    reads=[("modsb",)], writes=[("AV",)])

        def load_x(b, t0):
            for tg in range(T // 128):
                s = tg % 2
                S.op("sp", lambda e, s=s, tg=tg: e.dma_start(
                    out=xstage[:, s, :], in_=x_d[b, t0 + tg * 128:t0 + (tg + 1) * 128, :]),
                    writes=[("xstage", s)], src="dma_x%d" % s, inc=16)
                for half in range(2):
                    pb, pk = bankA() if half == 0 else bankB()

                    def fn(pe, pb=pb, s=s, half=half):
                        ins = None
                        for i in range(4):
                            ins = pe.transpose(out=pb[:, i * 128:(i + 1) * 128],
                                               in_=xstage[:, s, (half * 4 + i) * 128:(half * 4 + i + 1) * 128],
                                               identity=identf)
                        return ins
                    S.op("pe", fn, reads=[("xstage", s), ("consts",)], writes=[pk])
                    copy_alt(xT[:, half * 4:(half + 1) * 4, tg * 128:(tg + 1) * 128],
                             pb[:].rearrange("p (i n) -> p i n", i=4),
                             reads=[pk], writes=[("x", c, tg // 4) for c in range(half * 4, half * 4 + 4)])

        def rms_stats(tb):
            for c in range(KC):
                i = nxt("sq", 3)
                act_op(sq[:, i, :], xT[:, c, tsl(tb)], AF.Square, reads=[("x", c, tb)], writes=[("sq", i)])
                S.op("pe", lambda pe, i=i, c=c: pe.matmul(pbank[4][:], onesb[:], sq[:, i, :],
                                                          start=(c == 0), stop=(c == KC - 1)),
                     reads=[("sq", i), ("onesb",)], writes=[("ps", 4)] + [("po", o_) for o_ in range(4)])
            act_op(sd[:, tb, :], pbank[4][:], AF.Sqrt, reads=[("ps", 4), ("eps",)], writes=[("sd", tb)],
                   bias=vecs_eps, scale=1.0)
            S.op("dve", lambda e: e.reciprocal(out=rstd[:, tb, :], in_=sd[:, tb, :]),
                 reads=[("sd", tb)], writes=[("rstd", tb)])

        def norm_h(sub, b):
            for tb in range(NTB):
                rms_stats(tb)
            for tb in range(NTB):
                for c in range(KC):
                    i = nxt("tmp", 4)
                    tt("dve", tmpf[:, i, :], xT[:, c, tsl(tb)], rstd[:, tb, :], ALU.mult,
                       reads=[("x", c, tb), ("rstd", tb)], writes=[("tmp", i)])
                    act_op(bufH[:, c, tsl(tb)], tmpf[:, i, :], AF.Identity,
                           reads=[("tmp", i), ("AV",)], writes=[("H", c, tb)],
                           scale=AV[:, sub, 0, c, b:b + 1], bias=AV[:, sub, 1, c, b:b + 1])

        def actbuf(j):
            return (bufA, bufB, bufC)[j // 8], ("ABC"[j // 8], j % 8)

        def ffn(sub, b, tag):
            norm_h(sub, b)
            for j in range(FC):
                wg, wgk = w_take(("g" + tag, j))
                wu, wuk = w_take(("u" + tag, j))
                ab, (an, ac) = actbuf(j)
                for tb in range(NTB):
                    pg, pgk = bankA()
                    pu, puk = bankB()
                    hk = [("H", c, tb) for c in range(KC)]
                    mm_group(pg[:], [(wg[:, c * 128:(c + 1) * 128], bufH[:, c, tsl(tb)]) for c in range(KC)],
                             reads=[wgk] + hk, writes=[pgk])
                    mm_group(pu[:], [(wu[:, c * 128:(c + 1) * 128], bufH[:, c, tsl(tb)]) for c in range(KC)],
                             reads=[wuk] + hk, writes=[puk])
                    i = nxt("tmp", 4)
                    act_op(tmpf[:, i, :], pg[:], AF.Silu, reads=[pgk], writes=[("tmp", i)])
                    tt("dve", ab[:, ac, tsl(tb)], pu[:], tmpf[:, i, :], ALU.mult,
                       reads=[puk, ("tmp", i)], writes=[(an, ac, tb)])
            for m in range(KC):
                wd, wdk = w_take(("d" + tag, m))
                for tb in range(NTB):
                    po, pok = bankA()
                    pairs, rk = [], [wdk]
                    for j in range(FC):
                        ab, (an, ac) = actbuf(j)
                        pairs.append((wd[:, j * 128:(j + 1) * 128], ab[:, ac, tsl(tb)]))
                        rk.append((an, ac, tb))
                    mm_group(po[:], pairs, reads=rk, writes=[pok])
                    stt(xT[:, m, tsl(tb)], po[:], AV[:, sub, 2, m, b:b + 1], xT[:, m, tsl(tb)],
                        ALU.mult, ALU.add, reads=[pok, ("x", m, tb), ("AV",)], writes=[("x", m, tb)])

        def proj(wv_, wk_, src_buf, src_name, tb):
            return ([(wv_[:, k * 128:(k + 1) * 128], src_buf[:, k, tsl(tb)]) for k in range(KC)],
                    [wk_] + [(src_name, k, tb) for k in range(KC)])

        def mixer(b, ti):
            first = (ti == 0)
            norm_h(1, b)
            if first:
                S.op("dve", lambda e: e.memset(uhalo[:], 0.0), writes=[("uhalo", c) for c in range(KC)])
            for c in range(KC):
                wa, wak = w_take(("ca", c))
                wb, wbk = w_take(("cb", c))
                ub = c % 2
                S.op("dve", lambda e, ub=ub, c=c: e.tensor_copy(out=ubuf[:, ub, 0:30], in_=uhalo[:, c, :]),
                     reads=[("uhalo", c)], writes=[("u", ub, "h")])
                for tb in range(NTB):
                    pa, pak = bankA()
                    pb, pbk = bankB()
                    pr, rk = proj(wa, wak, bufH, "H", tb)
                    mm_group(pa[:], pr, reads=rk, writes=[pak])
                    pr, rk = proj(wb, wbk, bufH, "H", tb)
                    mm_group(pb[:], pr, reads=rk, writes=[pbk])
                    i = nxt("tmp", 4)
                    act_op(tmpf[:, i, :], pb[:], AF.Sigmoid, reads=[pbk], writes=[("tmp", i)])
                    tt("dve", ubuf[:, ub, 30 + tb * 512:30 + (tb + 1) * 512], pa[:], tmpf[:, i, :], ALU.mult,
                       reads=[pak, ("tmp", i)], writes=[("u", ub, tb)])
                S.op("dve", lambda e, ub=ub, c=c: e.tensor_copy(out=uhalo[:, c, :], in_=ubuf[:, ub, T:T + 30]),
                     reads=[("u", ub, 1)], writes=[("uhalo", c)])
                wsl = vecs[:, V_WDW + c * CONVW:V_WDW + (c + 1) * CONVW]
                tt("dve", diag[:, ub, :, :], identb[:].unsqueeze(1).to_broadcast([128, CONVW, 128]),
                   wsl.unsqueeze(2).to_broadcast([128, CONVW, 128]), ALU.mult,
                   reads=[("identb",), ("vecs",)], writes=[("diag", ub)])
                for tb in range(NTB):
                    py, pyk = bankA()
                    pairs = [(diag[:, ub, k, :], ubuf[:, ub, tb * 512 + k:tb * 512 + k + 512]) for k in range(CONVW)]
                    rk = [("diag", ub), ("u", ub, tb), ("u", ub, "h") if tb == 0 else ("u", ub, tb - 1)]
                    mm_group(py[:], pairs, reads=rk, writes=[pyk])
                    act_op(bufA[:, c, tsl(tb)], py[:], AF.Identity, reads=[pyk, ("vecs",)],
                           writes=[("A", c, tb)], bias=vecs[:, V_BDW + c:V_BDW + c + 1], scale=1.0)
            for tb in range(NTB):
                for c in range(KC):
                    i = nxt("sq", 3)
                    act_op(sq[:, i, :], bufA[:, c, tsl(tb)], AF.Square, reads=[("A", c, tb)], writes=[("sq", i)])
                    S.op("pe", lambda pe, c=c, tb=tb: pe.matmul(pbank[4][:], onesb[:], bufA[:, c, tsl(tb)],
                                                                 start=(c == 0), stop=(c == KC - 1)),
                         reads=[("A", c, tb), ("onesb",)], writes=[("ps", 4)])
                    S.op("pe", lambda pe, c=c, i=i: pe.matmul(pbank[5][:], onesb[:], sq[:, i, :],
                                                               start=(c == 0), stop=(c == KC - 1)),
                         reads=[("sq", i), ("onesb",)], writes=[("ps", 5)])
                i = nxt("tmp", 4)
                act_op(tmpf[:, i, :], pbank[4][:], AF.Square, reads=[("ps", 4)], writes=[("tmp", i)])
                i2 = nxt("tmp", 4)
                tt("dve", tmpf[:, i2, :], pbank[5][:], tmpf[:, i, :], ALU.subtract,
                   reads=[("ps", 5), ("tmp", i)], writes=[("tmp", i2)])
                act_op(sd[:, 0, :], tmpf[:, i2, :], AF.Sqrt, reads=[("tmp", i2), ("eps",)], writes=[("sd", 0)],
                       bias=vecs_eps, scale=1.0)
                S.op("dve", lambda e: e.reciprocal(out=sd[:, 1, :], in_=sd[:, 0, :]),
                     reads=[("sd", 0)], writes=[("sd", 1)])
                S.op("act", lambda e: e.copy(out=lmean[:], in_=pbank[4][:]), reads=[("ps", 4)], writes=[("lmean",)])
                for c in range(KC):
                    i = nxt("tmp", 4)
                    tt("dve", tmpf[:, i, :], bufA[:, c, tsl(tb)], lmean[:], ALU.subtract,
                       reads=[("A", c, tb), ("lmean",)], writes=[("tmp", i)])
                    i2 = nxt("tmp", 4)
                    tt("dve", tmpf[:, i2, :], tmpf[:, i, :], sd[:, 1, :], ALU.mult,
                       reads=[("tmp", i), ("sd", 1)], writes=[("tmp", i2)])
                    act_op(bufB[:, c, tsl(tb)], tmpf[:, i2, :], AF.Silu, reads=[("tmp", i2), ("vecs",)],
                           writes=[("B", c, tb)], scale=vecs[:, V_LNG + c:V_LNG + c + 1],
                           bias=vecs[:, V_LNB + c:V_LNB + c + 1])
            for m in range(KC):
                wgc, wgck = w_take(("gc", m))
                wco, wcok = w_take(("co", m))
                for tb in range(NTB):
                    p1, p1k = bankA()
                    p2, p2k = bankB()
                    pr, rk = proj(wgc, wgck, bufH, "H", tb)
                    mm_group(p1[:], pr, reads=rk, writes=[p1k])
                    pr, rk = proj(wco, wcok, bufB, "B", tb)
                    mm_group(p2[:], pr, reads=rk, writes=[p2k])
                    i = nxt("tmp", 4)
                    act_op(tmpf[:, i, :], p1[:], AF.Sigmoid, reads=[p1k], writes=[("tmp", i)])
                    tt("dve", bufC[:, m, tsl(tb)], p2[:], tmpf[:, i, :], ALU.mult,
                       reads=[p2k, ("tmp", i)], writes=[("C", m, tb)])
            if first:
                S.op("dve", lambda e: e.memset(kdup[:, :, 0:128], 0.0), writes=[("k", 0, "h"), ("k", 1, "h")])
                S.op("dve", lambda e: e.memset(vtok[:, 0, :], 0.0), writes=[("v", 0)])
            else:
                S.op("dve", lambda e: e.tensor_copy(out=kdup[:, :, 0:128], in_=kdup[:, :, T:T + 128]),
                     reads=[("k", 0, 1), ("k", 1, 1)], writes=[("k", 0, "h"), ("k", 1, "h")])
                S.op("dve", lambda e: e.tensor_copy(out=vtok[:, 0, :], in_=vtok[:, T // 128, :]),
                     reads=[("v", T // 128)], writes=[("v", 0)])
            for c in range(KC):
                wq, wqk = w_take(("q", c))
                for tb in range(NTB):
                    p1, p1k = bankA()
                    pr, rk = proj(wq, wqk, bufH, "H", tb)
                    mm_group(p1[:], pr, reads=rk, writes=[p1k])
                    copy_alt(bufA[:, c, tsl(tb)], p1[:], reads=[p1k], writes=[("A", c, tb)])
            for g in range(2):
                wk_v, wk_k = w_take(("k", g))
                for tb in range(NTB):
                    p1, p1k = bankB()
                    pr, rk = proj(wk_v, wk_k, bufH, "H", tb)
                    mm_group(p1[:], pr, reads=rk, writes=[p1k])
                    copy_alt(kdup[:, g, 128 + tb * 512:128 + (tb + 1) * 512], p1[:], reads=[p1k],
                             writes=[("k", g, tb)])
            wvv, wvk = w_take(("v", 0))
            for half in range(2):
                p1, p1k = bankA()

                def fn(pe, p1=p1, half=half):
                    ins = None
                    for blk in range(4):
                        tk = half * 4 + blk
                        for k in range(KC):
                            ins = pe.matmul(p1[:, blk * 128:(blk + 1) * 128], bufH[:, k, tk * 128:(tk + 1) * 128],
                                            wvv[:, k * 128:(k + 1) * 128], start=(k == 0), stop=(k == KC - 1))
                    return ins
                S.op("pe", fn, reads=[wvk] + [("H", k, half) for k in range(KC)], writes=[p1k])
                copy_alt(vtok[:, 1 + half * 4:1 + half * 4 + 4, :], p1[:].rearrange("p (a n) -> p a n", a=4),
                         reads=[p1k], writes=[("v", 1 + half * 4 + a) for a in range(4)])
            un = 0
            for qb in range(T // 128):
                tb = qb // 4
                mi = 1 if (first and qb == 0) else 0
                for c in range(KC):
                    g = c // 4
                    u2 = un % 2
                    un += 1
                    sbk = 5 + u2
                    Sb = pbank[sbk]
                    kkeys = [("k", g, "h")] if qb == 0 else []
                    kkeys += [("k", g, t_) for t_ in sorted({max(qb - 1, 0) // 4, qb // 4})]

                    def fn(pe, Sb=Sb, c=c, g=g, qb=qb, mi=mi):
                        ins = None
                        for hh in range(2):
                            pe.matmul(Sb[:, hh * 256:(hh + 1) * 256],
                                      bufA[hh * 64:(hh + 1) * 64, c, qb * 128:(qb + 1) * 128],
                                      kdup[hh * 64:(hh + 1) * 64, g, qb * 128:qb * 128 + 256],
                                      start=True, stop=False)
                            ins = pe.matmul(Sb[:, hh * 256:(hh + 1) * 256], identb[:], maskb[:, mi, :],
                                            start=False, stop=True)
                        return ins
                    S.op("pe", fn, reads=[("A", c, tb), ("identb",), ("maskb",)] + kkeys, writes=[("ps", sbk)])
                    st_ = sstat[:, u2]
                    Sv = Sb[:].rearrange("p (h s) -> p h s", h=2)
                    S.op("dve", lambda e, st_=st_, Sv=Sv: e.tensor_reduce(out=st_[:, 0, :], in_=Sv, axis=AX.X, op=ALU.max),
                         reads=[("ps", sbk)], writes=[("ss", u2, 0)])
                    stt(st_[:, 1, :], st_[:, 0, :], 0.125, vecs[:, V_SINK + 2 * c:V_SINK + 2 * c + 2],
                        ALU.mult, ALU.max, reads=[("ss", u2, 0), ("vecs",)], writes=[("ss", u2, 1)])
                    S.op("dve", lambda e, st_=st_: e.tensor_scalar(out=st_[:, 2, :], in0=st_[:, 1, :], scalar1=-1.0,
                                                                   scalar2=None, op0=ALU.mult),
                         reads=[("ss", u2, 1)], writes=[("ss", u2, 2)])
                    tt("dve", st_[:, 3, :], vecs[:, V_SINK + 2 * c:V_SINK + 2 * c + 2], st_[:, 1, :], ALU.subtract,
                       reads=[("ss", u2, 1), ("vecs",)], writes=[("ss", u2, 3)])
                    for hh in range(2):
                        act_op(E_sb[:, u2, hh, :], Sb[:, hh * 256:(hh + 1) * 256], AF.Exp,
                               reads=[("ps", sbk), ("ss", u2, 2)], writes=[("E", u2, hh), ("ss", u2, 4, hh)],
                               bias=st_[:, 2, hh:hh + 1], scale=0.125, accum_out=st_[:, 4, hh:hh + 1])
                    act_op(st_[:, 5, :], st_[:, 3, :], AF.Exp, reads=[("ss", u2, 3)], writes=[("ss", u2, 5)])
                    tt("dve", st_[:, 6, :], st_[:, 4, :], st_[:, 5, :], ALU.add,
                       reads=[("ss", u2, 4, 0), ("ss", u2, 4, 1), ("ss", u2, 5)], writes=[("ss", u2, 6)])
                    S.op("dve", lambda e, st_=st_: e.reciprocal(out=st_[:, 7, :], in_=st_[:, 6, :]),
                         reads=[("ss", u2, 6)], writes=[("ss", u2, 7)])
                    tt("dve", Pn[:, u2], E_sb[:, u2], st_[:, 7, :].unsqueeze(2).to_broadcast([128, 2, 256]), ALU.mult,
                       reads=[("E", u2, 0), ("E", u2, 1), ("ss", u2, 7)], writes=[("Pn", u2)])

                    def fnT(pe, u2=u2):
                        ins = None
                        for hh in range(2):
                            for sblk in range(2):
                                ins = pe.transpose(out=ptb[:, u2, hh * 2 + sblk, :],
                                                   in_=Pn[:, u2, hh, sblk * 128:(sblk + 1) * 128], identity=identb[:])
                        return ins
                    S.op("pe", fnT, reads=[("Pn", u2), ("identb",)], writes=[("pt", u2)])
                    copy_alt(PTs[:, u2], ptb[:, u2], reads=[("pt", u2)], writes=[("PTs", u2)])
                    osl = un % 4
                    Ob = pbank[4][:, osl * 128:(osl + 1) * 128]

                    def fnPV(pe, Ob=Ob, u2=u2, qb=qb, g=g):
                        ins = None
                        for hh in range(2):
                            for sblk in range(2):
                                ins = pe.matmul(Ob[hh * 64:(hh + 1) * 64, :], vtok[:, qb + sblk, g * 64:(g + 1) * 64],
                                                PTs[:, u2, hh * 2 + sblk, :], start=(sblk == 0), stop=(sblk == 1))
                        return ins
                    S.op("pe", fnPV, reads=[("PTs", u2), ("v", qb), ("v", qb + 1)], writes=[("po", osl), ("ps", 4)])
                    copy_alt(bufB[:, c, qb * 128:(qb + 1) * 128], Ob, reads=[("po", osl)], writes=[("B", c, tb)])
            if "dbg" in stages:
                S.source("dma_dbg")
                S.op("sp", lambda e: e.dma_start(out=dbg_d[0], in_=bufA[:]), reads=[("A", c, tb) for c in range(KC) for tb in range(NTB)], src="dma_dbg", inc=16)
                S.op("sp", lambda e: e.dma_start(out=dbg_d[1], in_=bufB[:]), reads=[("B", c, tb) for c in range(KC) for tb in range(NTB)], src="dma_dbg", inc=16)
                S.op("sp", lambda e: e.dma_start(out=dbg_d[2][:, 0:2, :], in_=kdup[:, :, 128:128 + T]), reads=[("k", g, tb) for g in range(2) for tb in range(NTB)], src="dma_dbg", inc=16)
                S.op("sp", lambda e: e.dma_start(out=dbg_d[3][:, 0:1, :].rearrange("p a (s n) -> p (a s) n", n=128), in_=vtok[:, 1:9, :]), reads=[("v", i_) for i_ in range(1, 9)], src="dma_dbg", inc=16)
            for m in range(KC):
                wga, wgak = w_take(("ga", m))
                wao, waok = w_take(("ao", m))
                for tb in range(NTB):
                    p1, p1k = bankA()
                    p2, p2k = bankB()
                    pr, rk = proj(wga, wgak, bufH, "H", tb)
                    mm_group(p1[:], pr, reads=rk, writes=[p1k])
                    pr, rk = proj(wao, waok, bufB, "B", tb)
                    mm_group(p2[:], pr, reads=rk, writes=[p2k])
                    i = nxt("tmp", 4)
                    act_op(tmpf[:, i, :], p1[:], AF.Sigmoid, reads=[p1k], writes=[("tmp", i)])
                    i2 = nxt("tmp", 4)
                    tt("dve", tmpf[:, i2, :], p2[:], tmpf[:, i, :], ALU.mult,
                       reads=[p2k, ("tmp", i)], writes=[("tmp", i2)])
                    if "noattn" in stages:
                        continue
                    if "noconv" in stages:
                        S.op("dve", lambda e, m=m, tb=tb, i2=i2: e.tensor_copy(out=bufC[:, m, tsl(tb)], in_=tmpf[:, i2, :]),
                             reads=[("tmp", i2)], writes=[("C", m, tb)])
                        continue
                    tt("dve", bufC[:, m, tsl(tb)], bufC[:, m, tsl(tb)], tmpf[:, i2, :], ALU.add,
                       reads=[("C", m, tb), ("tmp", i2)], writes=[("C", m, tb)])
            if "dbg" in stages:
                S.op("sp", lambda e: e.dma_start(out=dbg_d[2], in_=bufC[:]), reads=[("C", c, tb) for c in range(KC) for tb in range(NTB)], src="dma_dbg", inc=16)
            for m in range(KC):
                wo, wok = w_take(("wo", m))
                for tb in range(NTB):
                    po, pok = bankA()
                    pr, rk = proj(wo, wok, bufC, "C", tb)
                    mm_group(po[:], pr, reads=rk, writes=[pok])
                    stt(xT[:, m, tsl(tb)], po[:], AV[:, 1, 2, m, b:b + 1], xT[:, m, tsl(tb)],
                        ALU.mult, ALU.add, reads=[pok, ("x", m, tb), ("AV",)], writes=[("x", m, tb)])

        def skip_weights(descs):
            for d_ in descs:
                w_take(d_)

        def store_out(b, t0, do_norm):
            for tb in range(NTB):
                if do_norm:
                    rms_stats(tb)
                    for c in range(KC):
                        stt(xT[:, c, tsl(tb)], xT[:, c, tsl(tb)], vecs[:, V_GF + c:V_GF + c + 1], rstd[:, tb, :],
                            ALU.mult, ALU.mult, reads=[("x", c, tb), ("rstd", tb), ("vecs",)], writes=[("x", c, tb)])
                for g4 in range(4):
                    tg = tb * 4 + g4
                    os_ = tg % 2
                    for half in range(2):
                        pb, pk = bankA() if half == 0 else bankB()

                        def fn(pe, pb=pb, half=half, tg=tg):
                            ins = None
                            for i in range(4):
                                ins = pe.transpose(out=pb[:, i * 128:(i + 1) * 128],
                                                   in_=xT[:, half * 4 + i, tg * 128:(tg + 1) * 128], identity=identf)
                            return ins
                        S.op("pe", fn, reads=[("x", half * 4 + i, tb) for i in range(4)] + [("consts",)], writes=[pk])
                        copy_alt(ostage[:, os_, half * 512:(half + 1) * 512], pb[:], reads=[pk],
                                 writes=[("ostage", os_, half)])
                    S.op("sp", lambda e, os_=os_, tg=tg: e.dma_start(
                        out=out_d[b, t0 + tg * 128:t0 + (tg + 1) * 128, :], in_=ostage[:, os_, :]),
                        reads=[("ostage", os_, 0), ("ostage", os_, 1)], src="dma_o%d" % os_, inc=16)

        epst = sb("epst", [128, 1], F32)
        S.op("dve", lambda e: e.memset(epst[:], EPS), writes=[("eps",)])
        vecs_eps = epst[:, 0:1]

        for p in range(npass):
            b = p // NPASS_PER_SEQ
            ti = p % NPASS_PER_SEQ
            t0 = ti * T
            load_x(b, t0)
            if "ffn1" in stages:
                ffn(0, b, "1")
            else:
                skip_weights(plan[0:seg1] + [("d1", m) for m in range(8)])
            if "mixer" in stages:
                mixer(b, ti)
            else:
                skip_weights(plan[seg1:seg2])
            if "ffn2" in stages:
                ffn(2, b, "2")
            else:
                skip_weights(plan[seg2:] + [("d2", m) for m in range(8)])
            store_out(b, t0, "norm" in stages)
        S.final_wait("sp", ["dma_o0", "dma_o1"] + (["dma_dbg"] if "dbg" in stages else []))

        with nc.Block() as block:
            S.emit(block)
    return nc


def pack_inputs(x, c, w_ada, b_ada, norm_ffn1_g, ffn1_w_gate, ffn1_w_up, ffn1_w_down,
                norm_mix_g, w_in, attn_sinks, w_attn_o, conv_w_dw, conv_b_dw, conv_ln_g,
                conv_ln_b, w_conv_o, w_out, norm_ffn2_g, ffn2_w_gate, ffn2_w_up, ffn2_w_down,
                final_norm_g):
    f = lambda a: np.asarray(a, dtype=np.float32)
    x, c = f(x), f(c)
    w_ada, b_ada = f(w_ada)[0], f(b_ada)[0]
    w_in_ = f(w_in)[0]
    plan, seg1, seg2 = small_plan()
    srcs = {
        "g1": f(ffn1_w_gate)[0], "u1": f(ffn1_w_up)[0], "g2": f(ffn2_w_gate)[0], "u2": f(ffn2_w_up)[0],
        "q": w_in_[:, 0:1024], "ca": w_in_[:, 1280:2304], "cb": w_in_[:, 2304:3328],
        "ga": w_in_[:, 3328:4352], "gc": w_in_[:, 4352:5376],
        "ao": f(w_attn_o)[0], "co": f(w_conv_o)[0], "wo": f(w_out)[0],
    }
    wsm = np.empty((len(plan), 128, D), np.float32)
    for i, (kind, j) in enumerate(plan):
        if kind == "k":
            wk = w_in_[:, 1024 + j * 64:1024 + (j + 1) * 64]
            wsm[i] = _tile(np.concatenate([wk, wk], axis=1))
        elif kind == "v":
            wsm[i] = _tile(w_in_[:, 1152:1280])
        else:
            wsm[i] = _tile(srcs[kind][:, j * 128:(j + 1) * 128])
    wlg = np.empty((16, 128, DFF), np.float32)
    for s_, wd in enumerate((f(ffn1_w_down)[0], f(ffn2_w_down)[0])):
        for m in range(8):
            wlg[s_ * 8 + m] = _tile(wd[:, m * 128:(m + 1) * 128])
    wada = np.empty((72, 128, D), np.float32)
    for n in range(72):
        wada[n] = _tile(w_ada[:, n * 128:(n + 1) * 128])

    def pv(v):
        return f(v).reshape(-1)[:D].reshape(KC, 128).T

    consts = np.zeros((128, NCONST), np.float32)
    consts[:, C_ID:C_ID + 128] = np.eye(128, dtype=np.float32)
    consts[:, C_ON:C_ON + 128] = 1.0 / D
    qi = np.arange(128)[:, None]
    sj = np.arange(256)[None, :]
    rel = qi + 128 - sj
    valid = (rel >= 0) & (rel < 128)
    consts[:, C_M0:C_M0 + 256] = np.where(valid, 0.0, NEG)
    consts[:, C_M1:C_M1 + 256] = np.where(valid & (sj >= 128), 0.0, NEG)

    in_maps = []
    for core in range(NCORES):
        vecs = np.zeros((128, NV), np.float32)
        vecs[:, V_G1:V_G1 + 8] = pv(norm_ffn1_g)
        vecs[:, V_G2:V_G2 + 8] = pv(norm_mix_g)
        vecs[:, V_G3:V_G3 + 8] = pv(norm_ffn2_g)
        vecs[:, V_GF:V_GF + 8] = pv(final_norm_g)
        vecs[:, V_BDW:V_BDW + 8] = pv(conv_b_dw)
        vecs[:, V_LNG:V_LNG + 8] = pv(conv_ln_g)
        vecs[:, V_LNB:V_LNB + 8] = pv(conv_ln_b)
        vecs[:, V_BADA:V_BADA + 72] = b_ada.reshape(72, 128).T
        cc = c[core * BPC:(core + 1) * BPC]
        vecs[:, V_C:V_C + 16] = cc.reshape(BPC, KC, 128).transpose(2, 1, 0).reshape(128, 16)
        wdw = f(conv_w_dw)[0]
        vecs[:, V_WDW:V_WDW + 8 * CONVW] = wdw.reshape(CONVW, KC, 128).transpose(2, 1, 0).reshape(128, KC * CONVW)
        vecs[:, V_SINK:V_SINK + 16] = f(attn_sinks)[0][None, :]
        in_maps.append({
            "x": np.ascontiguousarray(x[core * BPC:(core + 1) * BPC]),
            "wsm": wsm, "wlg": wlg, "wada": wada, "vecs": vecs, "consts": consts,
        })
    return in_maps


_NC_CACHE = {}


def kernel(**inputs):
    in_maps = pack_inputs(**inputs)
    if "nc" not in _NC_CACHE:
        _NC_CACHE["nc"] = build_program()
    res = run_bass_kernel_spmd(_NC_CACHE["nc"], in_maps, core_ids=list(range(NCORES)))
    out = np.concatenate([np.asarray(r["out"]) for r in res.results], axis=0)
    return out.astype(np.float32)
```

```python
import contextlib
import numpy as np
import concourse.bass as bass
import concourse.mybir as mybir
from concourse.bass_utils import run_bass_kernel_spmd

F32 = mybir.dt.float32
BF16 = mybir.dt.bfloat16
AF = mybir.ActivationFunctionType
ALU = mybir.AluOpType
AX = mybir.AxisListType

NCORES = 8
D = 1024
KC = 8
DFF = 2816
FC = 22
SEQ = 2048
BPC = 2
T = 1024
NTB = 2
NPASS_PER_SEQ = SEQ // T
EPS = 1e-6
NSLOT = 4
CONVW = 31
NEG = -30000.0

V_G1, V_G2, V_G3, V_GF, V_BDW, V_LNG, V_LNB = 0, 8, 16, 24, 32, 40, 48
V_BADA = 56
V_C = V_BADA + 72
V_WDW = V_C + 16
V_SINK = V_WDW + 8 * CONVW
NV = V_SINK + 16
C_ID, C_ON, C_M0, C_M1 = 0, 128, 256, 512
NCONST = 768


def small_plan():
    p = []
    for j in range(FC):
        p += [("g1", j), ("u1", j)]
    seg1 = len(p)
    p += [("k", 0), ("k", 1), ("v", 0)]
    for c in range(KC):
        p += [("q", c)]
    for c in range(KC):
        p += [("ca", c), ("cb", c)]
    for m in range(KC):
        p += [("gc", m), ("co", m)]
    for m in range(KC):
        p += [("ga", m), ("ao", m)]
    for tb in range(NTB):
        for m in range(KC):
            p += [("wo", m, tb)]
    seg2 = len(p)
    for j in range(FC):
        p += [("g2", j), ("u2", j)]
    return p, seg1, seg2


def _tile(Wc):
    din = Wc.shape[0]
    return np.ascontiguousarray(Wc.reshape(din // 128, 128, 128).transpose(1, 0, 2)).reshape(128, din)


class Sched:
    ENG = ("pe", "act", "dve", "pool", "sp")

    def __init__(self, nc, stack):
        self.nc = nc
        self.stack = stack
        self.prog = {e: [] for e in self.ENG}
        self.sem = {}
        self.cnt = {}
        self.waited = {}
        self.lastw = {}
        self.readers = {}
        for e in ("pe", "act", "dve", "pool"):
            self.source(e)

    def source(self, name):
        if name not in self.sem:
            self.sem[name] = self.stack.enter_context(self.nc.semaphore("s_" + name))
            self.cnt[name] = 0
        return name

    def op(self, eng, fn, reads=(), writes=(), src=None, inc=1):
        source = src or eng
        writes = list(writes) + [k for k in reads if k[0] == "ps" and k not in writes]
        deps = {}

        def add(d, raw):
            s, v = d
            if s == eng and source == eng and not raw:
                return
            if deps.get(s, 0) < v:
                deps[s] = v

        for k in reads:
            if k in self.lastw:
                add(self.lastw[k], True)
        for k in writes:
            if k in self.lastw:
                add(self.lastw[k], False)
            for s, v in self.readers.get(k, {}).items():
                add((s, v), False)
        waits = []
        for s, v in deps.items():
            if self.waited.get((eng, s), 0) < v:
                self.waited[(eng, s)] = v
                waits.append((s, v))
        self.cnt[source] += inc
        v = self.cnt[source]
        self.prog[eng].append((waits, fn, source, inc))
        for k in writes:
            self.lastw[k] = (source, v)
            self.readers[k] = {}
        for k in reads:
            self.readers.setdefault(k, {})[source] = v
        return v

    def final_wait(self, eng, sources):
        waits = [(s, self.cnt[s]) for s in sources if self.cnt[s] > 0]
        self.prog[eng].append((waits, None, None, 0))

    def simulate(self):
        pc = {e: 0 for e in self.ENG}
        val = {s_: 0 for s_ in self.sem}
        progress = True
        while progress:
            progress = False
            for e in self.ENG:
                while pc[e] < len(self.prog[e]):
                    waits, fn, source, inc = self.prog[e][pc[e]]
                    if any(val[s_] < v for s_, v in waits):
                        break
                    if source is not None:
                        val[source] += inc
                    pc[e] += 1
                    progress = True
        stuck = {e: (pc[e], len(self.prog[e])) for e in self.ENG if pc[e] < len(self.prog[e])}
        for e, (p_, n_) in stuck.items():
            waits = self.prog[e][p_][0]
            print("STUCK", e, p_, n_, [(s_, v, val[s_]) for s_, v in waits])
        return not stuck

    def emit(self, block):
        def mk(e):
            def body(eng):
                for waits, fn, source, inc in self.prog[e]:
                    for s, v in waits:
                        eng.wait_ge(self.sem[s], v)
                    if fn is not None:
                        ins = fn(eng)
                        ins.then_inc(self.sem[source], inc)
            return body
        block.tensor(mk("pe"))
        block.scalar(mk("act"))
        block.vector(mk("dve"))
        block.gpsimd(mk("pool"))
        block.sync(mk("sp"))


def build_program(stages=("ffn1", "mixer", "ffn2", "norm"), npass=4):
    nc = bass.Bass("TRN2", target_bir_lowering=False)
    plan, seg1, seg2 = small_plan()
    NSM = len(plan)
    x_d = nc.dram_tensor("x", [BPC, SEQ, D], F32, kind="ExternalInput").ap()
    wsm_d = nc.dram_tensor("wsm", [NSM, 128, D], F32, kind="ExternalInput").ap()
    wlg_d = nc.dram_tensor("wlg", [16, 128, DFF], F32, kind="ExternalInput").ap()
    wada_d = nc.dram_tensor("wada", [72, 128, D], F32, kind="ExternalInput").ap()
    vecs_d = nc.dram_tensor("vecs", [128, NV], F32, kind="ExternalInput").ap()
    consts_d = nc.dram_tensor("consts", [128, NCONST], F32, kind="ExternalInput").ap()
    out_d = nc.dram_tensor("out", [BPC, SEQ, D], F32, kind="ExternalOutput").ap()
    if "dbg" in stages:
        dbg_d = nc.dram_tensor("dbg", [4, 128, KC, T], BF16, kind="ExternalOutput").ap()

    with contextlib.ExitStack() as st:
        def sb(name, shape, dt):
            return st.enter_context(nc.sbuf_tensor(name, shape, dt))

        def psum(name, shape, dt):
            return st.enter_context(nc.psum_tensor(name, shape, dt))

        xT = sb("xT", [128, KC, T], F32)
        bufH = sb("bufH", [128, KC, T], BF16)
        bufA = sb("bufA", [128, KC, T], BF16)
        bufB = sb("bufB", [128, KC, T], BF16)
        bufC = sb("bufC", [128, KC, T], BF16)
        wring = sb("wring", [128, NSLOT, 4 * D], BF16)
        xstage = sb("xstage", [128, 2, D], F32)
        ostage = sb("ostage", [128, 2, D], F32)
        vecs = sb("vecs_sb", [128, NV], F32)
        consts = sb("consts_sb", [128, NCONST], F32)
        identb = sb("identb", [128, 128], BF16)
        onesb = sb("onesb", [128, 128], BF16)
        maskb = sb("maskb", [128, 2, 256], BF16)
        cact = sb("cact", [128, KC, BPC], BF16)
        modsb = sb("modsb", [128, 72, BPC], F32)
        AV = sb("AV", [128, 3, 3, KC, BPC], F32)
        kdup = sb("kdup", [128, 2, 128 + T], BF16)
        vtok = sb("vtok", [128, T // 128 + 1, 128], BF16)
        ubuf = sb("ubuf", [128, 2, 30 + T], BF16)
        uhalo = sb("uhalo", [128, KC, 30], BF16)
        diag = sb("diag", [128, CONVW, 128], BF16)
        sq = sb("sq", [128, 3, 512], BF16)
        sd = sb("sd", [128, 2, 512], F32)
        rstd = sb("rstd", [128, NTB, 512], F32)
        tmpf = sb("tmpf", [128, 4, 512], F32)
        lmean = sb("lmean", [128, 512], F32)
        On = sb("On", [128, 3, 128], BF16)
        negsink = sb("negsink", [128, 16], F32)
        Pn = sb("Pn", [128, 3, 2, 256], BF16)
        PTs = sb("PTs", [128, 3, 4, 128], BF16)
        sstat = sb("sstat", [128, 3, 8, 2], F32)

        pbank = [psum("ps%d" % i, [128, 512], F32) for i in range(7)]
        ptb = psum("ps7", [128, 2, 4, 128], BF16)

        S = Sched(nc, st)
        for nm in ("dma_c", "dma_x0", "dma_x1", "dma_o0", "dma_o1"):
            S.source(nm)
        for s in range(NSLOT):
            S.source("dma_w%d" % s)

        groups = []
        for g0 in range(0, 72, 4):
            groups.append(("ada", g0, 4, [("ada", n) for n in range(g0, g0 + 4)]))
        for p in range(npass):
            def seg(a, b):
                i = a
                while i < b:
                    n = min(4, b - i)
                    groups.append(("sm", i, n, plan[i:i + n]))
                    i += n
            seg(0, seg1)
            for tb in range(NTB):
                for m in range(8):
                    groups.append(("lg", m, 1, [("d1", m, tb)]))
            seg(seg1, seg2)
            seg(seg2, NSM)
            for tb in range(NTB):
                for m in range(8):
                    groups.append(("lg", 8 + m, 1, [("d2", m, tb)]))
        wstate = {"issued": 0, "g": 0, "t": 0}

        def w_issue(upto):
            while wstate["issued"] < min(upto, len(groups)):
                gi = wstate["issued"]
                kind, start, n, _ = groups[gi]
                slot = gi % NSLOT
                if kind == "lg":
                    src_ap = wlg_d[start]
                    dst = wring[:, slot, 0:DFF]
                else:
                    dram = wada_d if kind == "ada" else wsm_d
                    src_ap = dram[start:start + n].rearrange("t p f -> p t f")
                    dst = wring[:, slot, 0:n * D].rearrange("p (t f) -> p t f", t=n)

                def fn(e, dst=dst, src_ap=src_ap):
                    return e.dma_start(out=dst, in_=src_ap)
                S.op("pool", fn, reads=(), writes=[("w", slot)], src="dma_w%d" % slot, inc=16)
                wstate["issued"] += 1

        def w_take(desc):
            gi, ti = wstate["g"], wstate["t"]
            kind, start, n, descs = groups[gi]
            assert descs[ti] == desc, (descs[ti], desc)
            w_issue(gi + NSLOT - 1)
            slot = gi % NSLOT
            if kind == "lg":
                view = wring[:, slot, 0:DFF]
            else:
                view = wring[:, slot, ti * D:(ti + 1) * D]
            wstate["t"] += 1
            if wstate["t"] == n:
                wstate["t"] = 0
                wstate["g"] += 1
            return view, ("w", slot)

        rr = {"sq": 0, "tmp": 0, "ev": 0, "mmA": 0, "mmB": 0}

        def nxt(name, n):
            v = rr[name]
            rr[name] = (v + 1) % n
            return v

        def mm_group(out_ap, pairs, reads, writes):
            def fn(pe):
                n = len(pairs)
                ins = None
                for i, (l, r) in enumerate(pairs):
                    ins = pe.matmul(out_ap, l, r, start=(i == 0), stop=(i == n - 1))
                return ins
            S.op("pe", fn, reads, writes)

        def act_op(out, in_, func, reads, writes, bias=None, scale=None, accum_out=None):
            kw = {}
            if bias is not None:
                kw["bias"] = bias
            if scale is not None:
                kw["scale"] = scale
            if accum_out is not None:
                kw["accum_out"] = accum_out
            S.op("act", lambda e: e.activation(out=out, in_=in_, func=func, **kw), reads, writes)

        def tt(eng, out, in0, in1, op, reads, writes):
            S.op(eng, lambda e: e.tensor_tensor(out=out, in0=in0, in1=in1, op=op), reads, writes)

        def stt(out, in0, scalar, in1, op0, op1, reads, writes):
            S.op("dve", lambda e: e.scalar_tensor_tensor(out=out, in0=in0, scalar=scalar, in1=in1,
                                                         op0=op0, op1=op1), reads, writes)

        def copy_alt(out, in_, reads, writes):
            if nxt("ev", 2) == 0:
                S.op("act", lambda e: e.copy(out=out, in_=in_), reads, writes)
            else:
                S.op("dve", lambda e: e.tensor_copy(out=out, in_=in_), reads, writes)

        def bankA():
            i = nxt("mmA", 2)
            return pbank[i], ("ps", i)

        def bankB():
            i = 2 + nxt("mmB", 2)
            return pbank[i], ("ps", i)

        def tsl(tb):
            return slice(tb * 512, (tb + 1) * 512)

        identf = consts[:, C_ID:C_ID + 128]

        S.op("sp", lambda e: e.dma_start(out=vecs[:], in_=vecs_d), writes=[("vecs",)], src="dma_c", inc=16)
        S.op("sp", lambda e: e.dma_start(out=consts[:], in_=consts_d), writes=[("consts",)], src="dma_c", inc=16)
        S.op("dve", lambda e: e.tensor_copy(out=identb[:], in_=consts[:, C_ID:C_ID + 128]),
             reads=[("consts",)], writes=[("identb",)])
        S.op("dve", lambda e: e.tensor_copy(out=onesb[:], in_=consts[:, C_ON:C_ON + 128]),
             reads=[("consts",)], writes=[("onesb",)])
        S.op("dve", lambda e: e.tensor_copy(out=maskb[:].rearrange("p a b -> p (a b)"),
                                            in_=consts[:, C_M0:C_M0 + 512]),
             reads=[("consts",)], writes=[("maskb",)])
        S.op("dve", lambda e: e.tensor_scalar(out=negsink[:], in0=vecs[:, V_SINK:V_SINK + 16], scalar1=-1.0,
                                              scalar2=None, op0=ALU.mult),
             reads=[("vecs",)], writes=[("negsink",)])
        act_op(cact[:].rearrange("p c b -> p (c b)"), vecs[:, V_C:V_C + 16], AF.Silu,
               reads=[("vecs",)], writes=[("cact",)])
        mod_ps = pbank[4]
        for n in range(72):
            wv_, wk_ = w_take(("ada", n))
            pairs = [(wv_[:, k * 128:(k + 1) * 128], cact[:, k, :]) for k in range(KC)]
            mm_group(mod_ps[:, n * 2:(n + 1) * 2], pairs, reads=[wk_, ("cact",)], writes=[("ps", 4)])
        tt("dve", modsb[:], mod_ps[:, 0:144].rearrange("p (n b) -> p n b", b=BPC),
           vecs[:, V_BADA:V_BADA + 72].unsqueeze(2).to_broadcast([128, 72, BPC]), ALU.add,
           reads=[("ps", 4), ("vecs",)], writes=[("modsb",)])
        for sub, (vg, gscale) in enumerate(((V_G1, 0.5), (V_G2, 1.0), (V_G3, 0.5))):
            for b in range(BPC):
                stt(AV[:, sub, 0, :, b], modsb[:, (3 * sub + 1) * 8:(3 * sub + 2) * 8, b], 1.0,
                    vecs[:, vg:vg + 8], ALU.add, ALU.mult, reads=[("modsb",), ("vecs",)], writes=[("AV",)])
                S.op("dve", lambda e, sub=sub, b=b: e.tensor_copy(
                    out=AV[:, sub, 1, :, b], in_=modsb[:, (3 * sub) * 8:(3 * sub + 1) * 8, b]),
                    reads=[("modsb",)], writes=[("AV",)])
                S.op("dve", lambda e, sub=sub, b=b, gscale=gscale: e.tensor_scalar(
                    out=AV[:, sub, 2, :, b], in0=modsb[:, (3 * sub + 2) * 8:(3 * sub + 3) * 8, b],
                    scalar1=gscale, scalar2=None, op0=ALU.mult),
                    reads=[("modsb",)], writes=[("AV",)])

        def load_x(b, t0, tb):
            for tg in range(tb * 4, tb * 4 + 4):
                s = tg % 2
                S.op("sp", lambda e, s=s, tg=tg: e.dma_start(
                    out=xstage[:, s, :], in_=x_d[b, t0 + tg * 128:t0 + (tg + 1) * 128, :]),
                    writes=[("xstage", s)], src="dma_x%d" % s, inc=16)
                for half in range(2):
                    pb, pk = bankA() if half == 0 else bankB()

                    def fn(pe, pb=pb, s=s, half=half):
                        ins = None
                        for i in range(4):
                            ins = pe.transpose(out=pb[:, i * 128:(i + 1) * 128],
                                               in_=xstage[:, s, (half * 4 + i) * 128:(half * 4 + i + 1) * 128],
                                               identity=identf)
                        return ins
                    S.op("pe", fn, reads=[("xstage", s), ("consts",)], writes=[pk])
                    copy_alt(xT[:, half * 4:(half + 1) * 4, tg * 128:(tg + 1) * 128],
                             pb[:].rearrange("p (i n) -> p i n", i=4),
                             reads=[pk], writes=[("x", c, tg // 4) for c in range(half * 4, half * 4 + 4)])

        def rms_stats(tb):
            for c in range(KC):
                i = nxt("sq", 3)
                act_op(sq[:, i, :], xT[:, c, tsl(tb)], AF.Square, reads=[("x", c, tb)], writes=[("sq", i)])
                S.op("pe", lambda pe, i=i, c=c: pe.matmul(pbank[4][:], onesb[:], sq[:, i, :],
                                                          start=(c == 0), stop=(c == KC - 1)),
                     reads=[("sq", i), ("onesb",)], writes=[("ps", 4)])
            act_op(sd[:, tb, :], pbank[4][:], AF.Sqrt, reads=[("ps", 4), ("eps",)], writes=[("sd", tb)],
                   bias=vecs_eps, scale=1.0)
            S.op("dve", lambda e: e.reciprocal(out=rstd[:, tb, :], in_=sd[:, tb, :]),
                 reads=[("sd", tb)], writes=[("rstd", tb)])

        def norm_h(sub, b, tbs=(0, 1)):
            for tb in tbs:
                rms_stats(tb)
                for c in range(KC):
                    i = nxt("tmp", 4)
                    tt("dve", tmpf[:, i, :], xT[:, c, tsl(tb)], rstd[:, tb, :], ALU.mult,
                       reads=[("x", c, tb), ("rstd", tb)], writes=[("tmp", i)])
                    act_op(bufH[:, c, tsl(tb)], tmpf[:, i, :], AF.Identity,
                           reads=[("tmp", i), ("AV",)], writes=[("H", c, tb)],
                           scale=AV[:, sub, 0, c, b:b + 1], bias=AV[:, sub, 1, c, b:b + 1])

        def actbuf(j):
            return (bufA, bufB, bufC)[j // 8], ("ABC"[j // 8], j % 8)

        def ffn(sub, b, tag, after):
            for j0 in range(0, FC, 2):
                ws = []
                for j in range(j0, min(j0 + 2, FC)):
                    ws.append((j, w_take(("g" + tag, j)), w_take(("u" + tag, j))))
                for tb in range(NTB):
                    for j, (wg, wgk), (wu, wuk) in ws:
                        ab, (an, ac) = actbuf(j)
                        pg, pgk = bankA()
                        pu, puk = bankB()
                        hk = [("H", c, tb) for c in range(KC)]
                        mm_group(pg[:], [(wg[:, c * 128:(c + 1) * 128], bufH[:, c, tsl(tb)]) for c in range(KC)],
                                 reads=[wgk] + hk, writes=[pgk])
                        mm_group(pu[:], [(wu[:, c * 128:(c + 1) * 128], bufH[:, c, tsl(tb)]) for c in range(KC)],
                                 reads=[wuk] + hk, writes=[puk])
                        i = nxt("tmp", 4)
                        act_op(tmpf[:, i, :], pg[:], AF.Silu, reads=[pgk], writes=[("tmp", i)])
                        tt("dve", ab[:, ac, tsl(tb)], pu[:], tmpf[:, i, :], ALU.mult,
                           reads=[puk, ("tmp", i)], writes=[(an, ac, tb)])
            for tb in range(NTB):
                for m in range(KC):
                    wd, wdk = w_take(("d" + tag, m, tb))
                    po, pok = bankA()
                    pairs, rk = [], [wdk]
                    for j in range(FC):
                        ab, (an, ac) = actbuf(j)
                        pairs.append((wd[:, j * 128:(j + 1) * 128], ab[:, ac, tsl(tb)]))
                        rk.append((an, ac, tb))
                    mm_group(po[:], pairs, reads=rk, writes=[pok])
                    stt(xT[:, m, tsl(tb)], po[:], AV[:, sub, 2, m, b:b + 1], xT[:, m, tsl(tb)],
                        ALU.mult, ALU.add, reads=[pok, ("x", m, tb), ("AV",)], writes=[("x", m, tb)])
                after(tb)

        def proj(wv_, wk_, src_buf, src_name, tb):
            return ([(wv_[:, k * 128:(k + 1) * 128], src_buf[:, k, tsl(tb)]) for k in range(KC)],
                    [wk_] + [(src_name, k, tb) for k in range(KC)])

        def mixer(b, ti, after):
            first = (ti == 0)
            NQB = T // 128
            if first:
                S.op("dve", lambda e: e.memset(kdup[:, :, 0:128], 0.0), writes=[("k", 0, "h"), ("k", 1, "h")])
                S.op("dve", lambda e: e.memset(vtok[:, 0, :], 0.0), writes=[("v", 0)])
                S.op("dve", lambda e: e.memset(uhalo[:], 0.0), writes=[("uhalo", c) for c in range(KC)])
            else:
                S.op("dve", lambda e: e.tensor_copy(out=kdup[:, :, 0:128], in_=kdup[:, :, T:T + 128]),
                     reads=[("k", 0, 1), ("k", 1, 1)], writes=[("k", 0, "h"), ("k", 1, "h")])
                S.op("dve", lambda e: e.tensor_copy(out=vtok[:, 0, :], in_=vtok[:, T // 128, :]),
                     reads=[("v", T // 128)], writes=[("v", 0)])
            wks = [w_take(("k", g)) for g in range(2)]
            for tb in range(NTB):
                for g in range(2):
                    wk_v, wk_k = wks[g]
                    p1, p1k = bankB()
                    pr, rk = proj(wk_v, wk_k, bufH, "H", tb)
                    mm_group(p1[:], pr, reads=rk, writes=[p1k])
                    copy_alt(kdup[:, g, 128 + tb * 512:128 + (tb + 1) * 512], p1[:], reads=[p1k],
                             writes=[("k", g, tb)])
            wvv, wvk = w_take(("v", 0))
            for half in range(2):
                p1, p1k = bankA()

                def fn(pe, p1=p1, half=half):
                    ins = None
                    for blk in range(4):
                        tk = half * 4 + blk
                        for k in range(KC):
                            ins = pe.matmul(p1[:, blk * 128:(blk + 1) * 128], bufH[:, k, tk * 128:(tk + 1) * 128],
                                            wvv[:, k * 128:(k + 1) * 128], start=(k == 0), stop=(k == KC - 1))
                    return ins
                S.op("pe", fn, reads=[wvk] + [("H", k, half) for k in range(KC)], writes=[p1k])
                copy_alt(vtok[:, 1 + half * 4:1 + half * 4 + 4, :], p1[:].rearrange("p (a n) -> p a n", a=4),
                         reads=[p1k], writes=[("v", 1 + half * 4 + a) for a in range(4)])
            for c0 in range(0, KC, 4):
                wqs = [(c, w_take(("q", c))) for c in range(c0, c0 + 4)]
                for tb in range(NTB):
                    for c, (wq, wqk) in wqs:
                        p1, p1k = bankA()
                        pr, rk = proj(wq, wqk, bufH, "H", tb)
                        mm_group(p1[:], pr, reads=rk, writes=[p1k])
                        copy_alt(bufA[:, c, tsl(tb)], p1[:], reads=[p1k],
                                 writes=[("A", c, tb)] + [("Aq", c, qb) for qb in range(tb * 4, tb * 4 + 4)])

            conv_steps = []

            def conv_ab(c):
                wa, wak = w_take(("ca", c))
                wb, wbk = w_take(("cb", c))
                ub = c % 2
                S.op("dve", lambda e, ub=ub, c=c: e.tensor_copy(out=ubuf[:, ub, 0:30], in_=uhalo[:, c, :]),
                     reads=[("uhalo", c)], writes=[("u", ub, "h")])
                for tb in range(NTB):
                    pa, pak = bankA()
                    pb, pbk = bankB()
                    pr, rk = proj(wa, wak, bufH, "H", tb)
                    mm_group(pa[:], pr, reads=rk, writes=[pak])
                    pr, rk = proj(wb, wbk, bufH, "H", tb)
                    mm_group(pb[:], pr, reads=rk, writes=[pbk])
                    i = nxt("tmp", 4)
                    act_op(tmpf[:, i, :], pb[:], AF.Sigmoid, reads=[pbk], writes=[("tmp", i)])
                    tt("dve", ubuf[:, ub, 30 + tb * 512:30 + (tb + 1) * 512], pa[:], tmpf[:, i, :], ALU.mult,
                       reads=[pak, ("tmp", i)], writes=[("u", ub, tb)])
                S.op("dve", lambda e, ub=ub, c=c: e.tensor_copy(out=uhalo[:, c, :], in_=ubuf[:, ub, T:T + 30]),
                     reads=[("u", ub, 1)], writes=[("uhalo", c)])

            def conv_dw(c, tb):
                ub = c % 2
                if tb == 0:
                    wsl = vecs[:, V_WDW + c * CONVW:V_WDW + (c + 1) * CONVW]
                    tt("dve", diag[:], identb[:].unsqueeze(1).to_broadcast([128, CONVW, 128]),
                       wsl.unsqueeze(2).to_broadcast([128, CONVW, 128]), ALU.mult,
                       reads=[("identb",), ("vecs",)], writes=[("diag",)])
                py, pyk = bankA()
                pairs = [(diag[:, k, :], ubuf[:, ub, tb * 512 + k:tb * 512 + k + 512]) for k in range(CONVW)]
                rk = [("diag",), ("u", ub, tb), ("u", ub, "h") if tb == 0 else ("u", ub, tb - 1)]
                mm_group(py[:], pairs, reads=rk, writes=[pyk])
                act_op(bufB[:, c, tsl(tb)], py[:], AF.Identity, reads=[pyk, ("vecs",)],
                       writes=[("B", c, tb)], bias=vecs[:, V_BDW + c:V_BDW + c + 1], scale=1.0)

            lnst = {}

            def ln_stats(tb, c):
                if c == 0:
                    lnst["m"] = bankA()
                    lnst["q"] = bankB()
                pm, pmk = lnst["m"]
                pq, pqk = lnst["q"]
                i = nxt("sq", 3)
                act_op(sq[:, i, :], bufB[:, c, tsl(tb)], AF.Square, reads=[("B", c, tb)], writes=[("sq", i)])
                S.op("pe", lambda pe, pm=pm, c=c, tb=tb: pe.matmul(pm[:], onesb[:], bufB[:, c, tsl(tb)],
                                                                   start=(c == 0), stop=(c == KC - 1)),
                     reads=[("B", c, tb), ("onesb",)], writes=[pmk])
                S.op("pe", lambda pe, pq=pq, c=c, i=i: pe.matmul(pq[:], onesb[:], sq[:, i, :],
                                                                 start=(c == 0), stop=(c == KC - 1)),
                     reads=[("sq", i), ("onesb",)], writes=[pqk])
                if c == KC - 1:
                    i = nxt("tmp", 4)
                    act_op(tmpf[:, i, :], pm[:], AF.Square, reads=[pmk], writes=[("tmp", i)])
                    i2 = nxt("tmp", 4)
                    tt("dve", tmpf[:, i2, :], pq[:], tmpf[:, i, :], ALU.subtract,
                       reads=[pqk, ("tmp", i)], writes=[("tmp", i2)])
                    act_op(sd[:, 0, :], tmpf[:, i2, :], AF.Sqrt, reads=[("tmp", i2), ("eps",)], writes=[("sd", 0)],
                           bias=vecs_eps, scale=1.0)
                    S.op("dve", lambda e: e.reciprocal(out=sd[:, 1, :], in_=sd[:, 0, :]),
                         reads=[("sd", 0)], writes=[("sd", 1)])
                    S.op("act", lambda e, pm=pm: e.copy(out=lmean[:], in_=pm[:]), reads=[pmk], writes=[("lmean",)])

            def ln_apply(tb, c):
                i = nxt("tmp", 4)
                tt("dve", tmpf[:, i, :], bufB[:, c, tsl(tb)], lmean[:], ALU.subtract,
                   reads=[("B", c, tb), ("lmean",)], writes=[("tmp", i)])
                i2 = nxt("tmp", 4)
                tt("dve", tmpf[:, i2, :], tmpf[:, i, :], sd[:, 1, :], ALU.mult,
                   reads=[("tmp", i), ("sd", 1)], writes=[("tmp", i2)])
                act_op(bufB[:, c, tsl(tb)], tmpf[:, i2, :], AF.Silu, reads=[("tmp", i2), ("vecs",)],
                       writes=[("B", c, tb)], scale=vecs[:, V_LNG + c:V_LNG + c + 1],
                       bias=vecs[:, V_LNB + c:V_LNB + c + 1])

            gcst = {}

            def conv_out(m, tb):
                if tb == 0:
                    gcst["gc"] = w_take(("gc", m))
                    gcst["co"] = w_take(("co", m))
                wgc, wgck = gcst["gc"]
                wco, wcok = gcst["co"]
                p1, p1k = bankA()
                p2, p2k = bankB()
                pr, rk = proj(wgc, wgck, bufH, "H", tb)
                mm_group(p1[:], pr, reads=rk, writes=[p1k])
                pr, rk = proj(wco, wcok, bufB, "B", tb)
                mm_group(p2[:], pr, reads=rk, writes=[p2k])
                i = nxt("tmp", 4)
                act_op(tmpf[:, i, :], p1[:], AF.Sigmoid, reads=[p1k], writes=[("tmp", i)])
                tt("dve", bufC[:, m, tsl(tb)], p2[:], tmpf[:, i, :], ALU.mult,
                   reads=[p2k, ("tmp", i)], writes=[("C", m, tb)])

            conv_steps.append((4.0, lambda: conv_ab(0)))
            for c in range(KC):
                if c + 1 < KC:
                    conv_steps.append((4.0, lambda c=c: conv_ab(c + 1)))
                for tb in range(NTB):
                    conv_steps.append((2.0, lambda c=c, tb=tb: conv_dw(c, tb)))
            for tb in range(NTB):
                for c in range(KC):
                    conv_steps.append((0.2, lambda c=c, tb=tb: ln_stats(tb, c)))
                for c in range(KC):
                    conv_steps.append((0.2, lambda c=c, tb=tb: ln_apply(tb, c)))
            for m in range(KC):
                for tb in range(NTB):
                    conv_steps.append((1.0, lambda m=m, tb=tb: conv_out(m, tb)))

            units = [(qb, c) for qb in range(NQB) for c in range(KC)]
            DEPTH = 3

            def st0(u):
                qb, c = units[u]
                g = c // 4
                d3 = u % DEPTH
                sbk = 5 + u % 2
                Sb = pbank[sbk]
                mi = 1 if (first and qb == 0) else 0
                kkeys = [("k", g, "h")] if qb == 0 else []
                kkeys += [("k", g, t_) for t_ in sorted({max(qb - 1, 0) // 4, qb // 4})]

                def fn(pe):
                    ins = None
                    for hh in range(2):
                        pe.matmul(Sb[:, hh * 256:(hh + 1) * 256],
                                  bufA[hh * 64:(hh + 1) * 64, c, qb * 128:(qb + 1) * 128],
                                  kdup[hh * 64:(hh + 1) * 64, g, qb * 128:qb * 128 + 256],
                                  start=True, stop=False)
                        ins = pe.matmul(Sb[:, hh * 256:(hh + 1) * 256], identb[:], maskb[:, mi, :],
                                        start=False, stop=True)
                    return ins
                S.op("pe", fn, reads=[("Aq", c, qb), ("identb",), ("maskb",)] + kkeys, writes=[("ps", sbk)])
                st_ = sstat[:, d3]
                Sv = Sb[:].rearrange("p (h s) -> p h s", h=2)
                S.op("dve", lambda e: e.tensor_reduce(out=st_[:, 0, :], in_=Sv, axis=AX.X, op=ALU.max),
                     reads=[("ps", sbk)], writes=[("ss", d3, 0)])
                stt(st_[:, 2, :], st_[:, 0, :], -0.125, negsink[:, 2 * c:2 * c + 2],
                    ALU.mult, ALU.min, reads=[("ss", d3, 0), ("negsink",)], writes=[("ss", d3, 2)])
                tt("dve", st_[:, 3, :], vecs[:, V_SINK + 2 * c:V_SINK + 2 * c + 2], st_[:, 2, :], ALU.add,
                   reads=[("ss", d3, 2), ("vecs",)], writes=[("ss", d3, 3)])
                for hh in range(2):
                    act_op(Pn[:, d3, hh, :], Sb[:, hh * 256:(hh + 1) * 256], AF.Exp,
                           reads=[("ps", sbk), ("ss", d3, 2)], writes=[("Pn", d3, hh), ("ss", d3, 4, hh)],
                           bias=st_[:, 2, hh:hh + 1], scale=0.125, accum_out=st_[:, 4, hh:hh + 1])
                act_op(st_[:, 5, :], st_[:, 3, :], AF.Exp, reads=[("ss", d3, 3)], writes=[("ss", d3, 5)])

            def st1(u):
                d3 = u % DEPTH
                st_ = sstat[:, d3]
                tt("dve", st_[:, 6, :], st_[:, 4, :], st_[:, 5, :], ALU.add,
                   reads=[("ss", d3, 4, 0), ("ss", d3, 4, 1), ("ss", d3, 5)], writes=[("ss", d3, 6)])
                S.op("dve", lambda e: e.reciprocal(out=st_[:, 7, :], in_=st_[:, 6, :]),
                     reads=[("ss", d3, 6)], writes=[("ss", d3, 7)])

            def st2(u):
                d3 = u % DEPTH

                def fnT(pe):
                    ins = None
                    for hh in range(2):
                        for sblk in range(2):
                            ins = pe.transpose(out=ptb[:, 0, hh * 2 + sblk, :],
                                               in_=Pn[:, d3, hh, sblk * 128:(sblk + 1) * 128], identity=identb[:])
                    return ins
                S.op("pe", fnT, reads=[("Pn", d3, 0), ("Pn", d3, 1), ("identb",)], writes=[("ps", 7)])
                S.op("act", lambda e: e.copy(out=PTs[:, d3], in_=ptb[:, 0]), reads=[("ps", 7)], writes=[("PTs", d3)])

            Ob = pbank[4][:, 0:128]
            OTb = pbank[4][:].bitcast(BF16)[:, 512:640]

            def st3(u):
                qb, c = units[u]
                g = c // 4
                d3 = u % DEPTH
                st_ = sstat[:, d3]

                def fnPV(pe):
                    ins = None
                    for hh in range(2):
                        for sblk in range(2):
                            ins = pe.matmul(Ob[:, hh * 64:(hh + 1) * 64], PTs[:, d3, hh * 2 + sblk, :],
                                            vtok[:, qb + sblk, g * 64:(g + 1) * 64],
                                            start=(sblk == 0), stop=(sblk == 1))
                    return ins
                S.op("pe", fnPV, reads=[("PTs", d3), ("v", qb), ("v", qb + 1)], writes=[("ps", 4)])
                tt("dve", On[:, d3, :].rearrange("p (h d) -> p h d", h=2), Ob.rearrange("p (h d) -> p h d", h=2),
                   st_[:, 7, :].unsqueeze(2).to_broadcast([128, 2, 64]), ALU.mult,
                   reads=[("ps", 4), ("ss", d3, 7)], writes=[("On", d3, 0), ("On", d3, 1)])

            def st4(u):
                qb, c = units[u]
                tb = qb // 4
                d3 = u % DEPTH
                S.op("pe", lambda pe: pe.transpose(out=OTb, in_=On[:, d3, :], identity=identb[:]),
                     reads=[("On", d3, 0), ("On", d3, 1), ("identb",)], writes=[("ps", 4)])
                S.op("dve", lambda e: e.tensor_copy(out=bufA[:, c, qb * 128:(qb + 1) * 128], in_=OTb),
                     reads=[("ps", 4)], writes=[("A", c, tb), ("Aq", c, qb)])

            NU = len(units)
            total_w = sum(w for w, _ in conv_steps)
            ci = 0
            cw = 0.0
            for it in range(NU + 4):
                if it < NU:
                    st0(it)
                if 0 <= it - 4 < NU:
                    st4(it - 4)
                target = total_w * min(1.0, (it + 1) / float(NU))
                if "seq" in stages:
                    target = total_w
                while ci < len(conv_steps) and cw + conv_steps[ci][0] * 0.5 <= target:
                    cw += conv_steps[ci][0]
                    conv_steps[ci][1]()
                    ci += 1
                for k_, f_ in ((1, st1), (2, st2), (3, st3)):
                    if 0 <= it - k_ < NU:
                        f_(it - k_)
            while ci < len(conv_steps):
                conv_steps[ci][1]()
                ci += 1

            if "dbg" in stages:
                S.source("dma_dbg")
                S.op("sp", lambda e: e.dma_start(out=dbg_d[1], in_=bufA[:]), reads=[("A", c, tb) for c in range(KC) for tb in range(NTB)], src="dma_dbg", inc=16)
            for m in range(KC):
                wga, wgak = w_take(("ga", m))
                wao, waok = w_take(("ao", m))
                for tb in range(NTB):
                    p1, p1k = bankA()
                    p2, p2k = bankB()
                    pr, rk = proj(wga, wgak, bufH, "H", tb)
                    mm_group(p1[:], pr, reads=rk, writes=[p1k])
                    pr, rk = proj(wao, waok, bufA, "A", tb)
                    mm_group(p2[:], pr, reads=rk, writes=[p2k])
                    i = nxt("tmp", 4)
                    act_op(tmpf[:, i, :], p1[:], AF.Sigmoid, reads=[p1k], writes=[("tmp", i)])
                    i2 = nxt("tmp", 4)
                    tt("dve", tmpf[:, i2, :], p2[:], tmpf[:, i, :], ALU.mult,
                       reads=[p2k, ("tmp", i)], writes=[("tmp", i2)])
                    if "noattn" in stages:
                        continue
                    if "noconv" in stages:
                        S.op("dve", lambda e, m=m, tb=tb, i2=i2: e.tensor_copy(out=bufC[:, m, tsl(tb)], in_=tmpf[:, i2, :]),
                             reads=[("tmp", i2)], writes=[("C", m, tb)])
                        continue
                    tt("dve", bufC[:, m, tsl(tb)], bufC[:, m, tsl(tb)], tmpf[:, i2, :], ALU.add,
                       reads=[("C", m, tb), ("tmp", i2)], writes=[("C", m, tb)])
            for tb in range(NTB):
                for m in range(KC):
                    wo, wok = w_take(("wo", m, tb))
                    po, pok = bankA()
                    pr, rk = proj(wo, wok, bufC, "C", tb)
                    mm_group(po[:], pr, reads=rk, writes=[pok])
                    stt(xT[:, m, tsl(tb)], po[:], AV[:, 1, 2, m, b:b + 1], xT[:, m, tsl(tb)],
                        ALU.mult, ALU.add, reads=[pok, ("x", m, tb), ("AV",)], writes=[("x", m, tb)])
                after(tb)

        def skip_weights(descs):
            for d_ in descs:
                w_take(d_)

        def store_out(b, t0, do_norm, tbs=(0, 1)):
            for tb in tbs:
                if do_norm:
                    rms_stats(tb)
                    for c in range(KC):
                        stt(xT[:, c, tsl(tb)], xT[:, c, tsl(tb)], vecs[:, V_GF + c:V_GF + c + 1], rstd[:, tb, :],
                            ALU.mult, ALU.mult, reads=[("x", c, tb), ("rstd", tb), ("vecs",)], writes=[("x", c, tb)])
                for g4 in range(4):
                    tg = tb * 4 + g4
                    os_ = tg % 2
                    for half in range(2):
                        pb, pk = bankA() if half == 0 else bankB()

                        def fn(pe, pb=pb, half=half, tg=tg):
                            ins = None
                            for i in range(4):
                                ins = pe.transpose(out=pb[:, i * 128:(i + 1) * 128],
                                                   in_=xT[:, half * 4 + i, tg * 128:(tg + 1) * 128], identity=identf)
                            return ins
                        S.op("pe", fn, reads=[("x", half * 4 + i, tb) for i in range(4)] + [("consts",)], writes=[pk])
                        copy_alt(ostage[:, os_, half * 512:(half + 1) * 512], pb[:], reads=[pk],
                                 writes=[("ostage", os_, half)])
                    S.op("sp", lambda e, os_=os_, tg=tg: e.dma_start(
                        out=out_d[b, t0 + tg * 128:t0 + (tg + 1) * 128, :], in_=ostage[:, os_, :]),
                        reads=[("ostage", os_, 0), ("ostage", os_, 1)], src="dma_o%d" % os_, inc=16)

        epst = sb("epst", [128, 1], F32)
        S.op("dve", lambda e: e.memset(epst[:], EPS), writes=[("eps",)])
        vecs_eps = epst[:, 0:1]

        assert set(("ffn1", "mixer", "ffn2")) <= set(stages) or npass >= 1
        for tb in range(NTB):
            load_x(0, 0, tb)
        norm_h(0, 0)
        for p in range(npass):
            b = p // NPASS_PER_SEQ
            ti = p % NPASS_PER_SEQ
            t0 = ti * T

            def after1(tb, b=b):
                norm_h(1, b, (tb,))

            def after2(tb, b=b):
                norm_h(2, b, (tb,))

            def after3(tb, b=b, t0=t0, p=p):
                store_out(b, t0, "norm" in stages, (tb,))
                if p + 1 < npass:
                    nb = (p + 1) // NPASS_PER_SEQ
                    nt0 = ((p + 1) % NPASS_PER_SEQ) * T
                    load_x(nb, nt0, tb)
                    norm_h(0, nb, (tb,))

            if "ffn1" in stages:
                ffn(0, b, "1", after1)
            else:
                skip_weights(plan[0:seg1] + [("d1", m, tb) for tb in range(NTB) for m in range(8)])
                for tb in range(NTB):
                    after1(tb)
            if "mixer" in stages:
                mixer(b, ti, after2)
            else:
                skip_weights(plan[seg1:seg2])
                for tb in range(NTB):
                    after2(tb)
            if "ffn2" in stages:
                ffn(2, b, "2", after3)
            else:
                skip_weights(plan[seg2:] + [("d2", m, tb) for tb in range(NTB) for m in range(8)])
                for tb in range(NTB):
                    after3(tb)
        S.final_wait("sp", ["dma_o0", "dma_o1"] + (["dma_dbg"] if "dbg" in stages else []))

        assert S.simulate(), "deadlock in schedule"
        with nc.Block() as block:
            S.emit(block)
    return nc


def pack_inputs(x, c, w_ada, b_ada, norm_ffn1_g, ffn1_w_gate, ffn1_w_up, ffn1_w_down,
                norm_mix_g, w_in, attn_sinks, w_attn_o, conv_w_dw, conv_b_dw, conv_ln_g,
                conv_ln_b, w_conv_o, w_out, norm_ffn2_g, ffn2_w_gate, ffn2_w_up, ffn2_w_down,
                final_norm_g):
    f = lambda a: np.asarray(a, dtype=np.float32)
    x, c = f(x), f(c)
    w_ada, b_ada = f(w_ada)[0], f(b_ada)[0]
    w_in_ = f(w_in)[0]
    plan, seg1, seg2 = small_plan()
    srcs = {
        "g1": f(ffn1_w_gate)[0], "u1": f(ffn1_w_up)[0], "g2": f(ffn2_w_gate)[0], "u2": f(ffn2_w_up)[0],
        "q": w_in_[:, 0:1024], "ca": w_in_[:, 1280:2304], "cb": w_in_[:, 2304:3328],
        "ga": w_in_[:, 3328:4352], "gc": w_in_[:, 4352:5376],
        "ao": f(w_attn_o)[0], "co": f(w_conv_o)[0], "wo": f(w_out)[0],
    }
    wsm = np.empty((len(plan), 128, D), np.float32)
    for i, desc in enumerate(plan):
        kind, j = desc[0], desc[1]
        if kind == "k":
            wk = w_in_[:, 1024 + j * 64:1024 + (j + 1) * 64]
            wsm[i] = _tile(np.concatenate([wk, wk], axis=1))
        elif kind == "v":
            wsm[i] = _tile(w_in_[:, 1152:1280])
        else:
            wsm[i] = _tile(srcs[kind][:, j * 128:(j + 1) * 128])
    wlg = np.empty((16, 128, DFF), np.float32)
    for s_, wd in enumerate((f(ffn1_w_down)[0], f(ffn2_w_down)[0])):
        for m in range(8):
            wlg[s_ * 8 + m] = _tile(wd[:, m * 128:(m + 1) * 128])
    wada = np.empty((72, 128, D), np.float32)
    for n in range(72):
        wada[n] = _tile(w_ada[:, n * 128:(n + 1) * 128])

    def pv(v):
        return f(v).reshape(-1)[:D].reshape(KC, 128).T

    consts = np.zeros((128, NCONST), np.float32)
    consts[:, C_ID:C_ID + 128] = np.eye(128, dtype=np.float32)
    consts[:, C_ON:C_ON + 128] = 1.0 / D
    qi = np.arange(128)[:, None]
    sj = np.arange(256)[None, :]
    rel = qi + 128 - sj
    valid = (rel >= 0) & (rel < 128)
    consts[:, C_M0:C_M0 + 256] = np.where(valid, 0.0, NEG)
    consts[:, C_M1:C_M1 + 256] = np.where(valid & (sj >= 128), 0.0, NEG)

    in_maps = []
    for core in range(NCORES):
        vecs = np.zeros((128, NV), np.float32)
        vecs[:, V_G1:V_G1 + 8] = pv(norm_ffn1_g)
        vecs[:, V_G2:V_G2 + 8] = pv(norm_mix_g)
        vecs[:, V_G3:V_G3 + 8] = pv(norm_ffn2_g)
        vecs[:, V_GF:V_GF + 8] = pv(final_norm_g)
        vecs[:, V_BDW:V_BDW + 8] = pv(conv_b_dw)
        vecs[:, V_LNG:V_LNG + 8] = pv(conv_ln_g)
        vecs[:, V_LNB:V_LNB + 8] = pv(conv_ln_b)
        vecs[:, V_BADA:V_BADA + 72] = b_ada.reshape(72, 128).T
        cc = c[core * BPC:(core + 1) * BPC]
        vecs[:, V_C:V_C + 16] = cc.reshape(BPC, KC, 128).transpose(2, 1, 0).reshape(128, 16)
        wdw = f(conv_w_dw)[0]
        vecs[:, V_WDW:V_WDW + 8 * CONVW] = wdw.reshape(CONVW, KC, 128).transpose(2, 1, 0).reshape(128, KC * CONVW)
        vecs[:, V_SINK:V_SINK + 16] = f(attn_sinks)[0][None, :]
        in_maps.append({
            "x": np.ascontiguousarray(x[core * BPC:(core + 1) * BPC]),
            "wsm": wsm, "wlg": wlg, "wada": wada, "vecs": vecs, "consts": consts,
        })
    return in_maps


_NC_CACHE = {}


def kernel(**inputs):
    in_maps = pack_inputs(**inputs)
    if "nc" not in _NC_CACHE:
        _NC_CACHE["nc"] = build_program()
    res = run_bass_kernel_spmd(_NC_CACHE["nc"], in_maps, core_ids=list(range(NCORES)))
    out = np.concatenate([np.asarray(r["out"]) for r in res.results], axis=0)
    return out.astype(np.float32)
```

```python
import contextlib
import numpy as np
import concourse.bass as bass
import concourse.mybir as mybir
from concourse.bass_utils import run_bass_kernel_spmd

F32 = mybir.dt.float32
BF16 = mybir.dt.bfloat16
AF = mybir.ActivationFunctionType
ALU = mybir.AluOpType
AX = mybir.AxisListType

NCORES = 8
D = 1024
KC = 8
DFF = 2816
FC = 22
SEQ = 2048
BPC = 2
T = 1024
NTB = 2
NPASS_PER_SEQ = SEQ // T
EPS = 1e-6
NSLOT = 4
CONVW = 31
NEG = -30000.0

V_G1, V_G2, V_G3, V_GF, V_BDW, V_LNG, V_LNB = 0, 8, 16, 24, 32, 40, 48
V_BADA = 56
V_C = V_BADA + 72
V_WDW = V_C + 16
V_SINK = V_WDW + 8 * CONVW
NV = V_SINK + 16
C_ID, C_ON, C_M0, C_M1 = 0, 128, 256, 512
C_MT0, C_MT1 = 768, 1024
NCONST = 1280


def small_plan():
    p = []
    for j in range(FC):
        p += [("g1", j), ("u1", j)]
    seg1 = len(p)
    p += [("k", 0), ("k", 1), ("v", 0)]
    for c in range(KC):
        p += [("q", c)]
    for c in range(KC):
        p += [("ca", c), ("cb", c)]
    for m in range(KC):
        p += [("gc", m), ("co", m)]
    for m in range(KC):
        p += [("ga", m), ("ao", m)]
    for tb in range(NTB):
        for m in range(KC):
            p += [("wo", m, tb)]
    seg2 = len(p)
    for j in range(FC):
        p += [("g2", j), ("u2", j)]
    return p, seg1, seg2


def _tile(Wc):
    din = Wc.shape[0]
    return np.ascontiguousarray(Wc.reshape(din // 128, 128, 128).transpose(1, 0, 2)).reshape(128, din)


class Sched:
    ENG = ("pe", "act", "dve", "pool", "sp")

    def __init__(self, nc, stack):
        self.nc = nc
        self.stack = stack
        self.prog = {e: [] for e in self.ENG}
        self.sem = {}
        self.cnt = {}
        self.waited = {}
        self.lastw = {}
        self.readers = {}
        for e in ("pe", "act", "dve", "pool"):
            self.source(e)

    def source(self, name):
        if name not in self.sem:
            self.sem[name] = self.stack.enter_context(self.nc.semaphore("s_" + name))
            self.cnt[name] = 0
        return name

    def op(self, eng, fn, reads=(), writes=(), src=None, inc=1):
        source = src or eng
        writes = list(writes) + [k for k in reads if k[0] == "ps" and k not in writes]
        deps = {}

        def add(d, raw):
            s, v = d
            if s == eng and source == eng and not raw:
                return
            if deps.get(s, 0) < v:
                deps[s] = v

        for k in reads:
            if k in self.lastw:
                add(self.lastw[k], True)
        for k in writes:
            if k in self.lastw:
                add(self.lastw[k], False)
            for s, v in self.readers.get(k, {}).items():
                add((s, v), False)
        waits = []
        for s, v in deps.items():
            if self.waited.get((eng, s), 0) < v:
                self.waited[(eng, s)] = v
                waits.append((s, v))
        self.cnt[source] += inc
        v = self.cnt[source]
        self.prog[eng].append((waits, fn, source, inc))
        for k in writes:
            self.lastw[k] = (source, v)
            self.readers[k] = {}
        for k in reads:
            self.readers.setdefault(k, {})[source] = v
        return v

    def final_wait(self, eng, sources):
        waits = [(s, self.cnt[s]) for s in sources if self.cnt[s] > 0]
        self.prog[eng].append((waits, None, None, 0))

    def simulate(self):
        pc = {e: 0 for e in self.ENG}
        val = {s_: 0 for s_ in self.sem}
        progress = True
        while progress:
            progress = False
            for e in self.ENG:
                while pc[e] < len(self.prog[e]):
                    waits, fn, source, inc = self.prog[e][pc[e]]
                    if any(val[s_] < v for s_, v in waits):
                        break
                    if source is not None:
                        val[source] += inc
                    pc[e] += 1
                    progress = True
        stuck = {e: (pc[e], len(self.prog[e])) for e in self.ENG if pc[e] < len(self.prog[e])}
        for e, (p_, n_) in stuck.items():
            waits = self.prog[e][p_][0]
            print("STUCK", e, p_, n_, [(s_, v, val[s_]) for s_, v in waits])
        return not stuck

    def emit(self, block):
        def mk(e):
            def body(eng):
                for waits, fn, source, inc in self.prog[e]:
                    for s, v in waits:
                        eng.wait_ge(self.sem[s], v)
                    if fn is not None:
                        ins = fn(eng)
                        ins.then_inc(self.sem[source], inc)
            return body
        block.tensor(mk("pe"))
        block.scalar(mk("act"))
        block.vector(mk("dve"))
        block.gpsimd(mk("pool"))
        block.sync(mk("sp"))


def build_program(stages=("ffn1", "mixer", "ffn2", "norm"), npass=4):
    nc = bass.Bass("TRN2", target_bir_lowering=False)
    plan, seg1, seg2 = small_plan()
    NSM = len(plan)
    x_d = nc.dram_tensor("x", [BPC, SEQ, D], F32, kind="ExternalInput").ap()
    wsm_d = nc.dram_tensor("wsm", [NSM, 128, D], F32, kind="ExternalInput").ap()
    wlg_d = nc.dram_tensor("wlg", [16, 128, DFF], F32, kind="ExternalInput").ap()
    wada_d = nc.dram_tensor("wada", [72, 128, D], F32, kind="ExternalInput").ap()
    vecs_d = nc.dram_tensor("vecs", [128, NV], F32, kind="ExternalInput").ap()
    consts_d = nc.dram_tensor("consts", [128, NCONST], F32, kind="ExternalInput").ap()
    out_d = nc.dram_tensor("out", [BPC, SEQ, D], F32, kind="ExternalOutput").ap()
    if "dbg" in stages:
        dbg_d = nc.dram_tensor("dbg", [4, 128, KC, T], BF16, kind="ExternalOutput").ap()

    with contextlib.ExitStack() as st:
        def sb(name, shape, dt):
            return st.enter_context(nc.sbuf_tensor(name, shape, dt))

        def psum(name, shape, dt):
            return st.enter_context(nc.psum_tensor(name, shape, dt))

        xT = sb("xT", [128, KC, T], F32)
        bufH = sb("bufH", [128, KC, T], BF16)
        bufA = sb("bufA", [128, KC, T], BF16)
        bufB = sb("bufB", [128, KC, T], BF16)
        bufC = sb("bufC", [128, KC, T], BF16)
        wring = sb("wring", [128, NSLOT, 4 * D], BF16)
        xstage = sb("xstage", [128, 2, D], F32)
        ostage = sb("ostage", [128, 2, D], F32)
        vecs = sb("vecs_sb", [128, NV], F32)
        consts = sb("consts_sb", [128, NCONST], F32)
        identb = sb("identb", [128, 128], BF16)
        onesb = sb("onesb", [128, 128], BF16)
        maskb = sb("maskb", [128, 2, 256], BF16)
        maskT = sb("maskT", [128, 2, 2, 256], BF16)
        cact = sb("cact", [128, KC, BPC], BF16)
        modsb = sb("modsb", [128, 72, BPC], F32)
        AV = sb("AV", [128, 3, 3, KC, BPC], F32)
        kdup = sb("kdup", [128, 2, 2, 128 + T], BF16)
        vtok = sb("vtok", [128, T // 128 + 1, 2, 65], BF16)
        gstat = sb("gstat", [128, 64], F32)
        ubuf = sb("ubuf", [128, 2, 30 + T], BF16)
        uhalo = sb("uhalo", [128, KC, 30], BF16)
        diag = sb("diag", [128, CONVW, 128], BF16)
        sq = sb("sq", [128, 3, 512], BF16)
        sd = sb("sd", [128, 2, 512], F32)
        rstd = sb("rstd", [128, NTB, 512], F32)
        tmpf = sb("tmpf", [128, 4, 512], F32)
        lmean = sb("lmean", [128, 512], F32)
        On = sb("On", [128, 3, 128], BF16)
        negsink = sb("negsink", [128, 16], F32)
        esink = sb("esink", [128, 16], F32)
        lnh = sb("lnh", [128, 16], F32)
        Pn = sb("Pn", [128, 3, 2, 256], BF16)
        PTs = sb("PTs", [128, 3, 4, 128], BF16)
        sstat = sb("sstat", [128, 3, 8, 2], F32)

        pbank = [psum("ps%d" % i, [128, 512], F32) for i in range(7)]
        ptb = psum("ps7", [128, 2, 4, 128], BF16)

        S = Sched(nc, st)
        for nm in ("dma_c", "dma_x0", "dma_x1", "dma_o0", "dma_o1"):
            S.source(nm)
        for s in range(NSLOT):
            S.source("dma_w%d" % s)

        groups = []
        for g0 in range(0, 72, 4):
            groups.append(("ada", g0, 4, [("ada", n) for n in range(g0, g0 + 4)]))
        for p in range(npass):
            def seg(a, b):
                i = a
                while i < b:
                    n = min(4, b - i)
                    groups.append(("sm", i, n, plan[i:i + n]))
                    i += n
            seg(0, seg1)
            for tb in range(NTB):
                for m in range(8):
                    groups.append(("lg", m, 1, [("d1", m, tb)]))
            seg(seg1, seg2)
            seg(seg2, NSM)
            for tb in range(NTB):
                for m in range(8):
                    groups.append(("lg", 8 + m, 1, [("d2", m, tb)]))
        wstate = {"issued": 0, "g": 0, "t": 0}

        def w_issue(upto):
            while wstate["issued"] < min(upto, len(groups)):
                gi = wstate["issued"]
                kind, start, n, _ = groups[gi]
                slot = gi % NSLOT
                if kind == "lg":
                    src_ap = wlg_d[start]
                    dst = wring[:, slot, 0:DFF]
                else:
                    dram = wada_d if kind == "ada" else wsm_d
                    src_ap = dram[start:start + n].rearrange("t p f -> p t f")
                    dst = wring[:, slot, 0:n * D].rearrange("p (t f) -> p t f", t=n)

                def fn(e, dst=dst, src_ap=src_ap):
                    return e.dma_start(out=dst, in_=src_ap)
                S.op("pool", fn, reads=(), writes=[("w", slot)], src="dma_w%d" % slot, inc=16)
                wstate["issued"] += 1

        def w_take(desc):
            gi, ti = wstate["g"], wstate["t"]
            kind, start, n, descs = groups[gi]
            assert descs[ti] == desc, (descs[ti], desc)
            w_issue(gi + NSLOT - 1)
            slot = gi % NSLOT
            if kind == "lg":
                view = wring[:, slot, 0:DFF]
            else:
                view = wring[:, slot, ti * D:(ti + 1) * D]
            wstate["t"] += 1
            if wstate["t"] == n:
                wstate["t"] = 0
                wstate["g"] += 1
            return view, ("w", slot)

        rr = {"sq": 0, "tmp": 0, "ev": 0, "mmA": 0, "mmB": 0}

        def nxt(name, n):
            v = rr[name]
            rr[name] = (v + 1) % n
            return v

        def mm_group(out_ap, pairs, reads, writes):
            def fn(pe):
                n = len(pairs)
                ins = None
                for i, (l, r) in enumerate(pairs):
                    ins = pe.matmul(out_ap, l, r, start=(i == 0), stop=(i == n - 1))
                return ins
            S.op("pe", fn, reads, writes)

        def act_op(out, in_, func, reads, writes, bias=None, scale=None, accum_out=None):
            kw = {}
            if bias is not None:
                kw["bias"] = bias
            if scale is not None:
                kw["scale"] = scale
            if accum_out is not None:
                kw["accum_out"] = accum_out
            S.op("act", lambda e: e.activation(out=out, in_=in_, func=func, **kw), reads, writes)

        def tt(eng, out, in0, in1, op, reads, writes):
            S.op(eng, lambda e: e.tensor_tensor(out=out, in0=in0, in1=in1, op=op), reads, writes)

        def stt(out, in0, scalar, in1, op0, op1, reads, writes):
            S.op("dve", lambda e: e.scalar_tensor_tensor(out=out, in0=in0, scalar=scalar, in1=in1,
                                                         op0=op0, op1=op1), reads, writes)

        def copy_alt(out, in_, reads, writes):
            if nxt("ev", 2) == 0:
                S.op("act", lambda e: e.copy(out=out, in_=in_), reads, writes)
            else:
                S.op("dve", lambda e: e.tensor_copy(out=out, in_=in_), reads, writes)

        def bankA():
            i = nxt("mmA", 2)
            return pbank[i], ("ps", i)

        bcfg = {"nB": 2}

        def bankB():
            if bcfg["nB"] == 1:
                return pbank[2], ("ps", 2)
            i = 2 + nxt("mmB", 2)
            return pbank[i], ("ps", i)

        def tsl(tb):
            return slice(tb * 512, (tb + 1) * 512)

        identf = consts[:, C_ID:C_ID + 128]

        S.op("sp", lambda e: e.dma_start(out=vecs[:], in_=vecs_d), writes=[("vecs",)], src="dma_c", inc=16)
        S.op("sp", lambda e: e.dma_start(out=consts[:], in_=consts_d), writes=[("consts",)], src="dma_c", inc=16)
        S.op("dve", lambda e: e.tensor_copy(out=identb[:], in_=consts[:, C_ID:C_ID + 128]),
             reads=[("consts",)], writes=[("identb",)])
        S.op("dve", lambda e: e.tensor_copy(out=onesb[:], in_=consts[:, C_ON:C_ON + 128]),
             reads=[("consts",)], writes=[("onesb",)])
        S.op("dve", lambda e: e.tensor_copy(out=maskb[:].rearrange("p a b -> p (a b)"),
                                            in_=consts[:, C_M0:C_M0 + 512]),
             reads=[("consts",)], writes=[("maskb",)])
        S.op("dve", lambda e: e.tensor_scalar(out=negsink[:], in0=vecs[:, V_SINK:V_SINK + 16], scalar1=-1.0,
                                              scalar2=None, op0=ALU.mult),
             reads=[("vecs",)], writes=[("negsink",)])
        for mi_ in range(2):
            for hh_ in range(2):
                S.op("dve", lambda e, mi_=mi_, hh_=hh_: e.tensor_copy(
                    out=maskT[:, mi_, hh_, :], in_=consts[:, C_MT0 + mi_ * 256:C_MT0 + (mi_ + 1) * 256]),
                    reads=[("consts",)], writes=[("maskT",)])
        S.op("dve", lambda e: e.memset(kdup[:].rearrange("p g h n -> p (g h n)"), 0.0),
             writes=[("k", g_, x_) for g_ in range(2) for x_ in ("h", 0, 1)] + [("k", g_, t_, 0) for g_ in range(2) for t_ in range(2)])
        S.op("dve", lambda e: e.memset(vtok[:, :, :, 64:65], 1.0), writes=[("vones",)])
        S.op("dve", lambda e: e.tensor_scalar(out=lnh[:], in0=vecs[:, V_LNG:V_LNG + 16], scalar1=0.5, scalar2=None,
                                              op0=ALU.mult),
             reads=[("vecs",)], writes=[("lnh",)])
        act_op(cact[:].rearrange("p c b -> p (c b)"), vecs[:, V_C:V_C + 16], AF.Silu,
               reads=[("vecs",)], writes=[("cact",)])
        mod_ps = pbank[4]
        for n in range(72):
            wv_, wk_ = w_take(("ada", n))
            pairs = [(wv_[:, k * 128:(k + 1) * 128], cact[:, k, :]) for k in range(KC)]
            mm_group(mod_ps[:, n * 2:(n + 1) * 2], pairs, reads=[wk_, ("cact",)], writes=[("ps", 4)])
        tt("dve", modsb[:], mod_ps[:, 0:144].rearrange("p (n b) -> p n b", b=BPC),
           vecs[:, V_BADA:V_BADA + 72].unsqueeze(2).to_broadcast([128, 72, BPC]), ALU.add,
           reads=[("ps", 4), ("vecs",)], writes=[("modsb",)])
        for sub, (vg, gscale) in enumerate(((V_G1, 0.5), (V_G2, 1.0), (V_G3, 0.5))):
            for b in range(BPC):
                stt(AV[:, sub, 0, :, b], modsb[:, (3 * sub + 1) * 8:(3 * sub + 2) * 8, b], 1.0,
                    vecs[:, vg:vg + 8], ALU.add, ALU.mult, reads=[("modsb",), ("vecs",)], writes=[("AV",)])
                S.op("dve", lambda e, sub=sub, b=b: e.tensor_copy(
                    out=AV[:, sub, 1, :, b], in_=modsb[:, (3 * sub) * 8:(3 * sub + 1) * 8, b]),
                    reads=[("modsb",)], writes=[("AV",)])
                S.op("dve", lambda e, sub=sub, b=b, gscale=gscale: e.tensor_scalar(
                    out=AV[:, sub, 2, :, b], in0=modsb[:, (3 * sub + 2) * 8:(3 * sub + 3) * 8, b],
                    scalar1=gscale, scalar2=None, op0=ALU.mult),
                    reads=[("modsb",)], writes=[("AV",)])

        def load_x(b, t0, tb):
            for tg in range(tb * 4, tb * 4 + 4):
                s = tg % 2
                S.op("sp", lambda e, s=s, tg=tg: e.dma_start(
                    out=xstage[:, s, :], in_=x_d[b, t0 + tg * 128:t0 + (tg + 1) * 128, :]),
                    writes=[("xstage", s)], src="dma_x%d" % s, inc=16)
                for half in range(2):
                    pb, pk = bankA() if half == 0 else bankB()

                    def fn(pe, pb=pb, s=s, half=half):
                        ins = None
                        for i in range(4):
                            ins = pe.transpose(out=pb[:, i * 128:(i + 1) * 128],
                                               in_=xstage[:, s, (half * 4 + i) * 128:(half * 4 + i + 1) * 128],
                                               identity=identf)
                        return ins
                    S.op("pe", fn, reads=[("xstage", s), ("consts",)], writes=[pk])
                    copy_alt(xT[:, half * 4:(half + 1) * 4, tg * 128:(tg + 1) * 128],
                             pb[:].rearrange("p (i n) -> p i n", i=4),
                             reads=[pk], writes=[("x", c, tg // 4) for c in range(half * 4, half * 4 + 4)])

        def rms_stats(tb):
            for c in range(KC):
                i = nxt("sq", 3)
                act_op(sq[:, i, :], xT[:, c, tsl(tb)], AF.Square, reads=[("x", c, tb)], writes=[("sq", i)])
                S.op("pe", lambda pe, i=i, c=c: pe.matmul(pbank[4][:], onesb[:], sq[:, i, :],
                                                          start=(c == 0), stop=(c == KC - 1)),
                     reads=[("sq", i), ("onesb",)], writes=[("ps", 4)])
            act_op(sd[:, tb, :], pbank[4][:], AF.Sqrt, reads=[("ps", 4), ("eps",)], writes=[("sd", tb)],
                   bias=vecs_eps, scale=1.0)
            S.op("dve", lambda e: e.reciprocal(out=rstd[:, tb, :], in_=sd[:, tb, :]),
                 reads=[("sd", tb)], writes=[("rstd", tb)])

        def norm_h(sub, b, tbs=(0, 1)):
            for tb in tbs:
                rms_stats(tb)
                for c in range(KC):
                    i = nxt("tmp", 4)
                    tt("dve", tmpf[:, i, :], xT[:, c, tsl(tb)], rstd[:, tb, :], ALU.mult,
                       reads=[("x", c, tb), ("rstd", tb)], writes=[("tmp", i)])
                    act_op(bufH[:, c, tsl(tb)], tmpf[:, i, :], AF.Identity,
                           reads=[("tmp", i), ("AV",)], writes=[("H", c, tb)],
                           scale=AV[:, sub, 0, c, b:b + 1], bias=AV[:, sub, 1, c, b:b + 1])

        def actbuf(j):
            return (bufA, bufB, bufC)[j // 8], ("ABC"[j // 8], j % 8)

        def ffn(sub, b, tag, after):
            for j0 in range(0, FC, 2):
                ws = []
                for j in range(j0, min(j0 + 2, FC)):
                    ws.append((j, w_take(("g" + tag, j)), w_take(("u" + tag, j))))
                for tb in range(NTB):
                    for j, (wg, wgk), (wu, wuk) in ws:
                        ab, (an, ac) = actbuf(j)
                        pg, pgk = bankA()
                        pu, puk = bankB()
                        hk = [("H", c, tb) for c in range(KC)]
                        mm_group(pg[:], [(wg[:, c * 128:(c + 1) * 128], bufH[:, c, tsl(tb)]) for c in range(KC)],
                                 reads=[wgk] + hk, writes=[pgk])
                        mm_group(pu[:], [(wu[:, c * 128:(c + 1) * 128], bufH[:, c, tsl(tb)]) for c in range(KC)],
                                 reads=[wuk] + hk, writes=[puk])
                        i = nxt("tmp", 4)
                        act_op(tmpf[:, i, :], pg[:], AF.Silu, reads=[pgk], writes=[("tmp", i)])
                        tt("dve", ab[:, ac, tsl(tb)], pu[:], tmpf[:, i, :], ALU.mult,
                           reads=[puk, ("tmp", i)], writes=[(an, ac, tb)])
            for tb in range(NTB):
                for m in range(KC):
                    wd, wdk = w_take(("d" + tag, m, tb))
                    po, pok = bankA()
                    pairs, rk = [], [wdk]
                    for j in range(FC):
                        ab, (an, ac) = actbuf(j)
                        pairs.append((wd[:, j * 128:(j + 1) * 128], ab[:, ac, tsl(tb)]))
                        rk.append((an, ac, tb))
                    mm_group(po[:], pairs, reads=rk, writes=[pok])
                    stt(xT[:, m, tsl(tb)], po[:], AV[:, sub, 2, m, b:b + 1], xT[:, m, tsl(tb)],
                        ALU.mult, ALU.add, reads=[pok, ("x", m, tb), ("AV",)], writes=[("x", m, tb)])
                after(tb)

        def proj(wv_, wk_, src_buf, src_name, tb):
            return ([(wv_[:, k * 128:(k + 1) * 128], src_buf[:, k, tsl(tb)]) for k in range(KC)],
                    [wk_] + [(src_name, k, tb) for k in range(KC)])

        def mixer(b, ti, after):
            first = (ti == 0)
            NQB = T // 128
            if first:
                S.op("dve", lambda e: e.memset(kdup[:].rearrange("p g h n -> p (g h) n")[:, :, 0:128], 0.0),
                     writes=[("k", 0, "h"), ("k", 1, "h")])
                S.op("dve", lambda e: e.memset(vtok[:, 0, :, 0:64], 0.0), writes=[("v", 0)])
                S.op("dve", lambda e: e.memset(uhalo[:], 0.0), writes=[("uhalo", c) for c in range(KC)])
            else:
                S.op("dve", lambda e: e.tensor_copy(out=kdup[:].rearrange("p g h n -> p (g h) n")[:, :, 0:128],
                                                    in_=kdup[:].rearrange("p g h n -> p (g h) n")[:, :, T:T + 128]),
                     reads=[("k", 0, 1), ("k", 1, 1)], writes=[("k", 0, "h"), ("k", 1, "h")])
                S.op("dve", lambda e: e.tensor_copy(out=vtok[:, 0, :, 0:64], in_=vtok[:, T // 128, :, 0:64]),
                     reads=[("v", T // 128)], writes=[("v", 0)])
            wks = [w_take(("k", g)) for g in range(2)]
            for tb in range(NTB):
                for g in range(2):
                    wk_v, wk_k = wks[g]
                    p1, p1k = bankB()
                    pr, rk = proj(wk_v, wk_k, bufH, "H", tb)
                    mm_group(p1[:], pr, reads=rk, writes=[p1k])
                    S.op("act", lambda e, g=g, tb=tb, p1=p1: e.copy(out=kdup[0:64, g, 0, 128 + tb * 512:128 + (tb + 1) * 512],
                                                                      in_=p1[0:64, :]),
                         reads=[p1k], writes=[("k", g, tb, 0)])
                    S.op("dve", lambda e, g=g, tb=tb, p1=p1: e.tensor_copy(out=kdup[64:128, g, 1, 128 + tb * 512:128 + (tb + 1) * 512],
                                                                             in_=p1[64:128, :]),
                         reads=[p1k], writes=[("k", g, tb)])
            wvv, wvk = w_take(("v", 0))
            for half in range(2):
                p1, p1k = bankA()

                def fn(pe, p1=p1, half=half):
                    ins = None
                    for blk in range(4):
                        tk = half * 4 + blk
                        for k in range(KC):
                            ins = pe.matmul(p1[:, blk * 128:(blk + 1) * 128], bufH[:, k, tk * 128:(tk + 1) * 128],
                                            wvv[:, k * 128:(k + 1) * 128], start=(k == 0), stop=(k == KC - 1))
                    return ins
                S.op("pe", fn, reads=[wvk] + [("H", k, half) for k in range(KC)], writes=[p1k])
                copy_alt(vtok[:, 1 + half * 4:1 + half * 4 + 4, :, 0:64],
                         p1[:].rearrange("p (a g n) -> p a g n", a=4, g=2),
                         reads=[p1k], writes=[("v", 1 + half * 4 + a) for a in range(4)])
            for c0 in range(0, KC, 4):
                wqs = [(c, w_take(("q", c))) for c in range(c0, c0 + 4)]
                for tb in range(NTB):
                    for c, (wq, wqk) in wqs:
                        p1, p1k = bankA()
                        pr, rk = proj(wq, wqk, bufH, "H", tb)
                        mm_group(p1[:], pr, reads=rk, writes=[p1k])
                        copy_alt(bufA[:, c, tsl(tb)], p1[:], reads=[p1k],
                                 writes=[("A", c, tb)] + [("Aq", c, qb) for qb in range(tb * 4, tb * 4 + 4)])

            conv_steps = []

            def conv_ab(c):
                wa, wak = w_take(("ca", c))
                wb, wbk = w_take(("cb", c))
                ub = c % 2
                S.op("dve", lambda e, ub=ub, c=c: e.tensor_copy(out=ubuf[:, ub, 0:30], in_=uhalo[:, c, :]),
                     reads=[("uhalo", c)], writes=[("u", ub, "h")])
                for tb in range(NTB):
                    pa, pak = bankA()
                    pb, pbk = bankB()
                    pr, rk = proj(wa, wak, bufH, "H", tb)
                    mm_group(pa[:], pr, reads=rk, writes=[pak])
                    pr, rk = proj(wb, wbk, bufH, "H", tb)
                    mm_group(pb[:], pr, reads=rk, writes=[pbk])
                    i = nxt("tmp", 4)
                    act_op(tmpf[:, i, :], pb[:], AF.Tanh, reads=[pbk], writes=[("tmp", i)], scale=0.5)
                    S.op("dve", lambda e, i=i: e.tensor_scalar(out=tmpf[:, i, :], in0=tmpf[:, i, :], scalar1=0.5, scalar2=0.5,
                                                               op0=ALU.mult, op1=ALU.add),
                         reads=[("tmp", i)], writes=[("tmp", i)])
                    tt("dve", ubuf[:, ub, 30 + tb * 512:30 + (tb + 1) * 512], pa[:], tmpf[:, i, :], ALU.mult,
                       reads=[pak, ("tmp", i)], writes=[("u", ub, tb)])
                S.op("dve", lambda e, ub=ub, c=c: e.tensor_copy(out=uhalo[:, c, :], in_=ubuf[:, ub, T:T + 30]),
                     reads=[("u", ub, 1)], writes=[("uhalo", c)])

            def conv_dw(c, tb):
                ub = c % 2
                if tb == 0:
                    wsl = vecs[:, V_WDW + c * CONVW:V_WDW + (c + 1) * CONVW]
                    tt("dve", diag[:], identb[:].unsqueeze(1).to_broadcast([128, CONVW, 128]),
                       wsl.unsqueeze(2).to_broadcast([128, CONVW, 128]), ALU.mult,
                       reads=[("identb",), ("vecs",)], writes=[("diag",)])
                py, pyk = bankA()
                pairs = [(diag[:, k, :], ubuf[:, ub, tb * 512 + k:tb * 512 + k + 512]) for k in range(CONVW)]
                rk = [("diag",), ("u", ub, tb), ("u", ub, "h") if tb == 0 else ("u", ub, tb - 1)]
                mm_group(py[:], pairs, reads=rk, writes=[pyk])
                act_op(bufB[:, c, tsl(tb)], py[:], AF.Identity, reads=[pyk, ("vecs",)],
                       writes=[("B", c, tb)], bias=vecs[:, V_BDW + c:V_BDW + c + 1], scale=1.0)

            lnst = {}

            def ln_stats(tb, c):
                if c == 0:
                    lnst["m"] = bankA()
                    lnst["q"] = bankB()
                pm, pmk = lnst["m"]
                pq, pqk = lnst["q"]
                i = nxt("sq", 3)
                act_op(sq[:, i, :], bufB[:, c, tsl(tb)], AF.Square, reads=[("B", c, tb)], writes=[("sq", i)])
                S.op("pe", lambda pe, pm=pm, c=c, tb=tb: pe.matmul(pm[:], onesb[:], bufB[:, c, tsl(tb)],
                                                                   start=(c == 0), stop=(c == KC - 1)),
                     reads=[("B", c, tb), ("onesb",)], writes=[pmk])
                S.op("pe", lambda pe, pq=pq, c=c, i=i: pe.matmul(pq[:], onesb[:], sq[:, i, :],
                                                                 start=(c == 0), stop=(c == KC - 1)),
                     reads=[("sq", i), ("onesb",)], writes=[pqk])
                if c == KC - 1:
                    i = nxt("tmp", 4)
                    act_op(tmpf[:, i, :], pm[:], AF.Square, reads=[pmk], writes=[("tmp", i)])
                    i2 = nxt("tmp", 4)
                    tt("dve", tmpf[:, i2, :], pq[:], tmpf[:, i, :], ALU.subtract,
                       reads=[pqk, ("tmp", i)], writes=[("tmp", i2)])
                    act_op(sd[:, 0, :], tmpf[:, i2, :], AF.Sqrt, reads=[("tmp", i2), ("eps",)], writes=[("sd", 0)],
                           bias=vecs_eps, scale=1.0)
                    S.op("dve", lambda e: e.reciprocal(out=sd[:, 1, :], in_=sd[:, 0, :]),
                         reads=[("sd", 0)], writes=[("sd", 1)])
                    S.op("act", lambda e, pm=pm: e.copy(out=lmean[:], in_=pm[:]), reads=[pmk], writes=[("lmean",)])

            def ln_apply(tb, c):
                i = nxt("tmp", 4)
                tt("dve", tmpf[:, i, :], bufB[:, c, tsl(tb)], lmean[:], ALU.subtract,
                   reads=[("B", c, tb), ("lmean",)], writes=[("tmp", i)])
                i2 = nxt("tmp", 4)
                tt("dve", tmpf[:, i2, :], tmpf[:, i, :], sd[:, 1, :], ALU.mult,
                   reads=[("tmp", i), ("sd", 1)], writes=[("tmp", i2)])
                i3 = nxt("tmp", 4)
                act_op(tmpf[:, i3, :], tmpf[:, i2, :], AF.Identity, reads=[("tmp", i2), ("lnh",)],
                       writes=[("tmp", i3)], scale=lnh[:, c:c + 1], bias=lnh[:, 8 + c:9 + c])
                act_op(tmpf[:, i, :], tmpf[:, i3, :], AF.Tanh, reads=[("tmp", i3)], writes=[("tmp", i)])
                stt(bufB[:, c, tsl(tb)], tmpf[:, i, :], 1.0, tmpf[:, i3, :], ALU.add, ALU.mult,
                    reads=[("tmp", i), ("tmp", i3)], writes=[("B", c, tb)])

            gcst = {}

            def conv_out(m, tb):
                if tb == 0:
                    gcst["gc"] = w_take(("gc", m))
                    gcst["co"] = w_take(("co", m))
                wgc, wgck = gcst["gc"]
                wco, wcok = gcst["co"]
                p1, p1k = bankA()
                p2, p2k = bankB()
                pr, rk = proj(wgc, wgck, bufH, "H", tb)
                mm_group(p1[:], pr, reads=rk, writes=[p1k])
                pr, rk = proj(wco, wcok, bufB, "B", tb)
                mm_group(p2[:], pr, reads=rk, writes=[p2k])
                i = nxt("tmp", 4)
                act_op(tmpf[:, i, :], p1[:], AF.Tanh, reads=[p1k], writes=[("tmp", i)], scale=0.5)
                S.op("dve", lambda e, i=i: e.tensor_scalar(out=tmpf[:, i, :], in0=tmpf[:, i, :], scalar1=0.5, scalar2=0.5,
                                                           op0=ALU.mult, op1=ALU.add),
                     reads=[("tmp", i)], writes=[("tmp", i)])
                tt("dve", bufC[:, m, tsl(tb)], p2[:], tmpf[:, i, :], ALU.mult,
                   reads=[p2k, ("tmp", i)], writes=[("C", m, tb)])

            conv_steps.append((7.0, lambda: conv_ab(0)))
            for c in range(KC):
                if c + 1 < KC:
                    conv_steps.append((7.0, lambda c=c: conv_ab(c + 1)))
                for tb in range(NTB):
                    conv_steps.append((6.7, lambda c=c, tb=tb: conv_dw(c, tb)))
            for tb in range(NTB):
                for c in range(KC):
                    conv_steps.append((1.5, lambda c=c, tb=tb: ln_stats(tb, c)))
                for c in range(KC):
                    conv_steps.append((3.0, lambda c=c, tb=tb: ln_apply(tb, c)))
            for m in range(KC):
                for tb in range(NTB):
                    conv_steps.append((3.5, lambda m=m, tb=tb: conv_out(m, tb)))

            units = [(qb, c) for qb in range(NQB) for c in range(KC)]
            DEPTH = 3

            kranges = ((0, 512), (512, 1024), (1024, 128 + T))
            for g in range(2):
                for r_, (c0_, c1_) in enumerate(kranges):
                    i = nxt("sq", 3)
                    n_ = c1_ - c0_
                    kk_ = [("k", g, "h"), ("k", g, 0), ("k", g, 1), ("k", g, 0, 0), ("k", g, 1, 0)]
                    act_op(sq[:, i, 0:n_], kdup[:, g, 0, c0_:c1_], AF.Square, reads=kk_, writes=[("sq", i)])
                    S.op("pe", lambda pe, i=i, n_=n_: pe.matmul(pbank[4][:, 0:n_], onesb[:], sq[:, i, 0:n_], start=True, stop=True),
                         reads=[("sq", i), ("onesb",)], writes=[("ps", 4)])
                    S.op("dve", lambda e, g=g, r_=r_, n_=n_: e.tensor_reduce(out=gstat[:, g * 3 + r_:g * 3 + r_ + 1], in_=pbank[4][:, 0:n_],
                                                                         axis=AX.X, op=ALU.max),
                         reads=[("ps", 4)], writes=[("gs", g * 3 + r_)])
                S.op("dve", lambda e, g=g: e.tensor_reduce(out=gstat[:, 6 + g:7 + g], in_=gstat[:, g * 3:g * 3 + 3], axis=AX.X, op=ALU.max),
                     reads=[("gs", g * 3 + r_) for r_ in range(3)], writes=[("gs", 6 + g)])
            for c in range(KC):
                for tb in range(NTB):
                    i = nxt("sq", 3)
                    act_op(sq[:, i, :], bufA[:, c, tsl(tb)], AF.Square, reads=[("A", c, tb)], writes=[("sq", i)])
                    S.op("pe", lambda pe, i=i: pe.matmul(pbank[4][:], onesb[:], sq[:, i, :], start=True, stop=True),
                         reads=[("sq", i), ("onesb",)], writes=[("ps", 4)])
                    S.op("dve", lambda e, c=c, tb=tb: e.tensor_reduce(out=gstat[:, 8 + c * 2 + tb:9 + c * 2 + tb], in_=pbank[4][:],
                                                                    axis=AX.X, op=ALU.max),
                         reads=[("ps", 4)], writes=[("gs", 8 + c * 2 + tb)])
                g = c // 4
                S.op("dve", lambda e, c=c: e.tensor_reduce(out=gstat[:, 24 + c:25 + c], in_=gstat[:, 8 + 2 * c:10 + 2 * c], axis=AX.X, op=ALU.max),
                     reads=[("gs", 8 + 2 * c), ("gs", 9 + 2 * c)], writes=[("gs", 24 + c)])
                tt("dve", gstat[:, 32 + c:33 + c], gstat[:, 24 + c:25 + c], gstat[:, 6 + g:7 + g], ALU.mult,
                   reads=[("gs", 24 + c), ("gs", 6 + g)], writes=[("gs", 32 + c)])
            for c in range(KC):
                act_op(gstat[:, 40 + c:41 + c], gstat[:, 32 + c:33 + c], AF.Sqrt, reads=[("gs", 32 + c)], writes=[("gs", 40 + c)])
            for c in range(KC):
                S.op("dve", lambda e, c=c: e.tensor_scalar(out=gstat[:, 48 + c:49 + c], in0=gstat[:, 40 + c:41 + c],
                                                           scalar1=-(1024.0 / 8.0) * 1.01, scalar2=None, op0=ALU.mult),
                     reads=[("gs", 40 + c)], writes=[("gs", 48 + c)])
                S.op("dve", lambda e, c=c: e.tensor_scalar(out=esink[:, 2 * c:2 * c + 2], in0=vecs[:, V_SINK + 2 * c:V_SINK + 2 * c + 2],
                                                           scalar1=gstat[:, 48 + c:49 + c], scalar2=None, op0=ALU.add),
                     reads=[("gs", 48 + c), ("vecs",)], writes=[("esd", c)])
            for c in range(KC):
                act_op(esink[:, 2 * c:2 * c + 2], esink[:, 2 * c:2 * c + 2], AF.Exp, reads=[("esd", c)], writes=[("es", c)])

            def st0(u):
                qb, c = units[u]
                g = c // 4
                d3 = u % DEPTH
                sbk = 5 + u % 2
                Sb = pbank[sbk]
                mi = 1 if (first and qb == 0) else 0
                kkeys = [("k", g, "h")] if qb == 0 else []
                for t_ in sorted({max(qb - 1, 0) // 4, qb // 4}):
                    kkeys += [("k", g, t_), ("k", g, t_, 0)]

                S.op("pe", lambda pe: pe.matmul(Sb[:].rearrange("p (h s) -> p h s", h=2), bufA[:, c, qb * 128:(qb + 1) * 128],
                                                kdup[:, g, :, qb * 128:qb * 128 + 256], start=True, stop=True),
                     reads=[("Aq", c, qb)] + kkeys, writes=[("ps", sbk)])
                act_op(Pn[:, d3].rearrange("p h s -> p (h s)"), Sb[:], AF.Exp,
                       reads=[("ps", sbk), ("gs", 48 + c)], writes=[("Pn", d3, 0), ("Pn", d3, 1)],
                       bias=gstat[:, 48 + c:49 + c], scale=0.125)
                Pf = Pn[:, d3].rearrange("p h s -> p (h s)")
                tt("dve", Pf, Pf, maskT[:, mi].rearrange("p h m -> p (h m)"), ALU.mult,
                   reads=[("Pn", d3, 0), ("Pn", d3, 1), ("maskT",)], writes=[("Pn", d3, 0), ("Pn", d3, 1)])

            def st1(u):
                pass

            def st2(u):
                d3 = u % DEPTH

                def fnT(pe):
                    ins = None
                    for hh in range(2):
                        for sblk in range(2):
                            ins = pe.transpose(out=ptb[:, 0, hh * 2 + sblk, :],
                                               in_=Pn[:, d3, hh, sblk * 128:(sblk + 1) * 128], identity=identb[:])
                    return ins
                S.op("pe", fnT, reads=[("Pn", d3, 0), ("Pn", d3, 1), ("identb",)], writes=[("ps", 7)])
                S.op("act", lambda e: e.copy(out=PTs[:, d3], in_=ptb[:, 0]), reads=[("ps", 7)], writes=[("PTs", d3)])

            Ob = pbank[4][:, 0:130]
            OTb = pbank[3][:].bitcast(BF16)[:, 0:128]

            def st3(u):
                qb, c = units[u]
                g = c // 4
                d3 = u % DEPTH
                st_ = sstat[:, d3]

                def fnPV(pe):
                    ins = None
                    for hh in range(2):
                        for sblk in range(2):
                            ins = pe.matmul(Ob[:, hh * 65:(hh + 1) * 65], PTs[:, d3, hh * 2 + sblk, :],
                                            vtok[:, qb + sblk, g, :],
                                            start=(sblk == 0), stop=(sblk == 1))
                    return ins
                S.op("pe", fnPV, reads=[("PTs", d3), ("v", qb), ("v", qb + 1), ("vones",)], writes=[("ps", 4)])
                Ov = Ob.rearrange("p (h d) -> p h d", h=2)
                tt("dve", st_[:, 6, :], Ov[:, :, 64], esink[:, 2 * c:2 * c + 2], ALU.add,
                   reads=[("ps", 4), ("es", c)], writes=[("ss", d3, 6)])
                S.op("dve", lambda e: e.reciprocal(out=st_[:, 7, :], in_=st_[:, 6, :]),
                     reads=[("ss", d3, 6)], writes=[("ss", d3, 7)])
                tt("dve", On[:, d3, :].rearrange("p (h d) -> p h d", h=2), Ov[:, :, 0:64],
                   st_[:, 7, :].unsqueeze(2).to_broadcast([128, 2, 64]), ALU.mult,
                   reads=[("ps", 4), ("ss", d3, 7)], writes=[("On", d3, 0), ("On", d3, 1)])

            def st4(u):
                qb, c = units[u]
                tb = qb // 4
                d3 = u % DEPTH
                S.op("pe", lambda pe: pe.transpose(out=OTb, in_=On[:, d3, :], identity=identb[:]),
                     reads=[("On", d3, 0), ("On", d3, 1), ("identb",)], writes=[("ps", 3)])
                S.op("dve", lambda e: e.tensor_copy(out=bufA[:, c, qb * 128:(qb + 1) * 128], in_=OTb),
                     reads=[("ps", 3)], writes=[("A", c, tb), ("Aq", c, qb)])

            NU = len(units)
            total_w = sum(w for w, _ in conv_steps)
            ci = 0
            cw = 0.0
            bcfg["nB"] = 1
            for it in range(NU + 5):
                if it < NU:
                    st0(it)
                if 0 <= it - 5 < NU:
                    st4(it - 5)
                target = total_w * min(1.0, (it + 1) / float(NU))
                if "seq" in stages:
                    target = total_w
                while ci < len(conv_steps) and cw + conv_steps[ci][0] * 0.5 <= target:
                    cw += conv_steps[ci][0]
                    conv_steps[ci][1]()
                    ci += 1
                for k_, f_ in ((1, st1), (2, st2), (3, st3)):
                    if 0 <= it - k_ < NU:
                        f_(it - k_)
            while ci < len(conv_steps):
                conv_steps[ci][1]()
                ci += 1
            bcfg["nB"] = 2

            if "dbg" in stages:
                S.source("dma_dbg")
                S.op("sp", lambda e: e.dma_start(out=dbg_d[1], in_=bufA[:]), reads=[("A", c, tb) for c in range(KC) for tb in range(NTB)], src="dma_dbg", inc=16)
            for m in range(KC):
                wga, wgak = w_take(("ga", m))
                wao, waok = w_take(("ao", m))
                for tb in range(NTB):
                    p1, p1k = bankA()
                    p2, p2k = bankB()
                    pr, rk = proj(wga, wgak, bufH, "H", tb)
                    mm_group(p1[:], pr, reads=rk, writes=[p1k])
                    pr, rk = proj(wao, waok, bufA, "A", tb)
                    mm_group(p2[:], pr, reads=rk, writes=[p2k])
                    i = nxt("tmp", 4)
                    act_op(tmpf[:, i, :], p1[:], AF.Sigmoid, reads=[p1k], writes=[("tmp", i)])
                    i2 = nxt("tmp", 4)
                    tt("dve", tmpf[:, i2, :], p2[:], tmpf[:, i, :], ALU.mult,
                       reads=[p2k, ("tmp", i)], writes=[("tmp", i2)])
                    if "noattn" in stages:
                        continue
                    if "noconv" in stages:
                        S.op("dve", lambda e, m=m, tb=tb, i2=i2: e.tensor_copy(out=bufC[:, m, tsl(tb)], in_=tmpf[:, i2, :]),
                             reads=[("tmp", i2)], writes=[("C", m, tb)])
                        continue
                    tt("dve", bufC[:, m, tsl(tb)], bufC[:, m, tsl(tb)], tmpf[:, i2, :], ALU.add,
                       reads=[("C", m, tb), ("tmp", i2)], writes=[("C", m, tb)])
            for tb in range(NTB):
                for m in range(KC):
                    wo, wok = w_take(("wo", m, tb))
                    po, pok = bankA()
                    pr, rk = proj(wo, wok, bufC, "C", tb)
                    mm_group(po[:], pr, reads=rk, writes=[pok])
                    stt(xT[:, m, tsl(tb)], po[:], AV[:, 1, 2, m, b:b + 1], xT[:, m, tsl(tb)],
                        ALU.mult, ALU.add, reads=[pok, ("x", m, tb), ("AV",)], writes=[("x", m, tb)])
                after(tb)

        def skip_weights(descs):
            for d_ in descs:
                w_take(d_)

        def store_out(b, t0, do_norm, tbs=(0, 1)):
            for tb in tbs:
                if do_norm:
                    rms_stats(tb)
                    for c in range(KC):
                        stt(xT[:, c, tsl(tb)], xT[:, c, tsl(tb)], vecs[:, V_GF + c:V_GF + c + 1], rstd[:, tb, :],
                            ALU.mult, ALU.mult, reads=[("x", c, tb), ("rstd", tb), ("vecs",)], writes=[("x", c, tb)])
                for g4 in range(4):
                    tg = tb * 4 + g4
                    os_ = tg % 2
                    for half in range(2):
                        pb, pk = bankA() if half == 0 else bankB()

                        def fn(pe, pb=pb, half=half, tg=tg):
                            ins = None
                            for i in range(4):
                                ins = pe.transpose(out=pb[:, i * 128:(i + 1) * 128],
                                                   in_=xT[:, half * 4 + i, tg * 128:(tg + 1) * 128], identity=identf)
                            return ins
                        S.op("pe", fn, reads=[("x", half * 4 + i, tb) for i in range(4)] + [("consts",)], writes=[pk])
                        copy_alt(ostage[:, os_, half * 512:(half + 1) * 512], pb[:], reads=[pk],
                                 writes=[("ostage", os_, half)])
                    S.op("sp", lambda e, os_=os_, tg=tg: e.dma_start(
                        out=out_d[b, t0 + tg * 128:t0 + (tg + 1) * 128, :], in_=ostage[:, os_, :]),
                        reads=[("ostage", os_, 0), ("ostage", os_, 1)], src="dma_o%d" % os_, inc=16)

        epst = sb("epst", [128, 1], F32)
        S.op("dve", lambda e: e.memset(epst[:], EPS), writes=[("eps",)])
        vecs_eps = epst[:, 0:1]

        assert set(("ffn1", "mixer", "ffn2")) <= set(stages) or npass >= 1
        for tb in range(NTB):
            load_x(0, 0, tb)
        norm_h(0, 0)
        for p in range(npass):
            b = p // NPASS_PER_SEQ
            ti = p % NPASS_PER_SEQ
            t0 = ti * T

            def after1(tb, b=b):
                norm_h(1, b, (tb,))

            def after2(tb, b=b):
                norm_h(2, b, (tb,))

            def after3(tb, b=b, t0=t0, p=p):
                store_out(b, t0, "norm" in stages, (tb,))
                if p + 1 < npass:
                    nb = (p + 1) // NPASS_PER_SEQ
                    nt0 = ((p + 1) % NPASS_PER_SEQ) * T
                    load_x(nb, nt0, tb)
                    norm_h(0, nb, (tb,))

            if "ffn1" in stages:
                ffn(0, b, "1", after1)
            else:
                skip_weights(plan[0:seg1] + [("d1", m, tb) for tb in range(NTB) for m in range(8)])
                for tb in range(NTB):
                    after1(tb)
            if "mixer" in stages:
                mixer(b, ti, after2)
            else:
                skip_weights(plan[seg1:seg2])
                for tb in range(NTB):
                    after2(tb)
            if "ffn2" in stages:
                ffn(2, b, "2", after3)
            else:
                skip_weights(plan[seg2:] + [("d2", m, tb) for tb in range(NTB) for m in range(8)])
                for tb in range(NTB):
                    after3(tb)
        S.final_wait("sp", ["dma_o0", "dma_o1"] + (["dma_dbg"] if "dbg" in stages else []))

        assert S.simulate(), "deadlock in schedule"
        with nc.Block() as block:
            S.emit(block)
    return nc


def pack_inputs(x, c, w_ada, b_ada, norm_ffn1_g, ffn1_w_gate, ffn1_w_up, ffn1_w_down,
                norm_mix_g, w_in, attn_sinks, w_attn_o, conv_w_dw, conv_b_dw, conv_ln_g,
                conv_ln_b, w_conv_o, w_out, norm_ffn2_g, ffn2_w_gate, ffn2_w_up, ffn2_w_down,
                final_norm_g):
    f = lambda a: np.asarray(a, dtype=np.float32)
    x, c = f(x), f(c)
    w_ada, b_ada = f(w_ada)[0], f(b_ada)[0]
    w_in_ = f(w_in)[0]
    plan, seg1, seg2 = small_plan()
    srcs = {
        "g1": f(ffn1_w_gate)[0], "u1": f(ffn1_w_up)[0], "g2": f(ffn2_w_gate)[0], "u2": f(ffn2_w_up)[0],
        "q": w_in_[:, 0:1024], "ca": w_in_[:, 1280:2304], "cb": w_in_[:, 2304:3328],
        "ga": w_in_[:, 3328:4352], "gc": w_in_[:, 4352:5376],
        "ao": f(w_attn_o)[0], "co": f(w_conv_o)[0], "wo": f(w_out)[0],
    }
    wsm = np.empty((len(plan), 128, D), np.float32)
    for i, desc in enumerate(plan):
        kind, j = desc[0], desc[1]
        if kind == "k":
            wk = w_in_[:, 1024 + j * 64:1024 + (j + 1) * 64]
            wsm[i] = _tile(np.concatenate([wk, wk], axis=1))
        elif kind == "v":
            wsm[i] = _tile(w_in_[:, 1152:1280])
        else:
            wsm[i] = _tile(srcs[kind][:, j * 128:(j + 1) * 128])
    wlg = np.empty((16, 128, DFF), np.float32)
    for s_, wd in enumerate((f(ffn1_w_down)[0], f(ffn2_w_down)[0])):
        for m in range(8):
            wlg[s_ * 8 + m] = _tile(wd[:, m * 128:(m + 1) * 128])
    wada = np.empty((72, 128, D), np.float32)
    for n in range(72):
        wada[n] = _tile(w_ada[:, n * 128:(n + 1) * 128])

    def pv(v):
        return f(v).reshape(-1)[:D].reshape(KC, 128).T

    consts = np.zeros((128, NCONST), np.float32)
    consts[:, C_ID:C_ID + 128] = np.eye(128, dtype=np.float32)
    consts[:, C_ON:C_ON + 128] = 1.0 / D
    qi = np.arange(128)[:, None]
    sj = np.arange(256)[None, :]
    rel = qi + 128 - sj
    valid = (rel >= 0) & (rel < 128)
    consts[:, C_M0:C_M0 + 256] = np.where(valid, 0.0, NEG)
    consts[:, C_M1:C_M1 + 256] = np.where(valid & (sj >= 128), 0.0, NEG)
    consts[:, C_MT0:C_MT0 + 256] = valid
    consts[:, C_MT1:C_MT1 + 256] = valid & (sj >= 128)

    in_maps = []
    for core in range(NCORES):
        vecs = np.zeros((128, NV), np.float32)
        vecs[:, V_G1:V_G1 + 8] = pv(norm_ffn1_g)
        vecs[:, V_G2:V_G2 + 8] = pv(norm_mix_g)
        vecs[:, V_G3:V_G3 + 8] = pv(norm_ffn2_g)
        vecs[:, V_GF:V_GF + 8] = pv(final_norm_g)
        vecs[:, V_BDW:V_BDW + 8] = pv(conv_b_dw)
        vecs[:, V_LNG:V_LNG + 8] = pv(conv_ln_g)
        vecs[:, V_LNB:V_LNB + 8] = pv(conv_ln_b)
        vecs[:, V_BADA:V_BADA + 72] = b_ada.reshape(72, 128).T
        cc = c[core * BPC:(core + 1) * BPC]
        vecs[:, V_C:V_C + 16] = cc.reshape(BPC, KC, 128).transpose(2, 1, 0).reshape(128, 16)
        wdw = f(conv_w_dw)[0]
        vecs[:, V_WDW:V_WDW + 8 * CONVW] = wdw.reshape(CONVW, KC, 128).transpose(2, 1, 0).reshape(128, KC * CONVW)
        vecs[:, V_SINK:V_SINK + 16] = f(attn_sinks)[0][None, :]
        in_maps.append({
            "x": np.ascontiguousarray(x[core * BPC:(core + 1) * BPC]),
            "wsm": wsm, "wlg": wlg, "wada": wada, "vecs": vecs, "consts": consts,
        })
    return in_maps


_NC_CACHE = {}


def kernel(**inputs):
    in_maps = pack_inputs(**inputs)
    if "nc" not in _NC_CACHE:
        _NC_CACHE["nc"] = build_program()
    res = run_bass_kernel_spmd(_NC_CACHE["nc"], in_maps, core_ids=list(range(NCORES)))
    out = np.concatenate([np.asarray(r["out"]) for r in res.results], axis=0)
    return out.astype(np.float32)
```
